# Optimizing a Trainium2 kernel written in Bass

```python
import jax, jax.numpy as jnp
from jax import lax
import numpy as np

D_MODEL = 2048
BATCH = 4
SEQ = 4096
DEPTH = 4

CHUNK = 64
RWKV_WIDTH = D_MODEL // 2
RWKV_HEAD = 64
RWKV_HEADS = RWKV_WIDTH // RWKV_HEAD
CONV_WIDTH = D_MODEL - RWKV_WIDTH
SHORT_CONV = 3
DECAY_LORA = 64
AAA_LORA = 64
MV_LORA = 32
GATE_LORA = 160
RWKV_COLS = 3 * RWKV_WIDTH + DECAY_LORA + AAA_LORA + GATE_LORA
EVEN_COLS = RWKV_COLS + 3 * CONV_WIDTH
GN_EPS = 64e-5
FOX_HEADS = 16
FOX_HEAD = D_MODEL // FOX_HEADS
Q_BLOCK = 128
ODD_COLS = 3 * D_MODEL + FOX_HEADS
D_FF = 5632
FFN_CONV = 3
LN_EPS = 1e-5
N_EVEN = (DEPTH + 1) // 2
N_ODD = DEPTH // 2
DN_ALPHA = (2 * DEPTH) ** 0.25
DN_BETA = (8 * DEPTH) ** -0.25

kernel_name = 'hybrid_rwkv7_shortconv_fox_convffn_deepnorm'


def layer_norm(x, g, b):
    xf = x.astype(jnp.float32)
    mu = jnp.mean(xf, axis=-1, keepdims=True)
    var = jnp.mean(jnp.square(xf - mu), axis=-1, keepdims=True)
    return ((xf - mu) * lax.rsqrt(var + LN_EPS) * g + b).astype(x.dtype)


def causal_dwconv(u, w, b=None):
    K, C = w.shape
    y = lax.conv_general_dilated(u, w[:, None, :].astype(u.dtype), window_strides=(1,),
                                 padding=[(K - 1, 0)], dimension_numbers=('NWC', 'WIO', 'NWC'),
                                 feature_group_count=C)
    if b is not None:
        y = y + b
    return y


def token_shift(u):
    return jnp.pad(u, ((0, 0), (1, 0), (0, 0)))[:, :-1]


def rwkv7_scan(r, w, k, v, a_vec, b_vec):
    Bsz, _, H, N = r.shape

    def step(S, inp):
        r_t, w_t, k_t, v_t, a_t, b_t = inp
        sa = jnp.einsum('bhij,bhj->bhi', S, a_t)
        S = S * w_t[:, :, None, :] + sa[..., :, None] * b_t[..., None, :] + v_t[..., :, None] * k_t[..., None, :]
        return S, jnp.einsum('bhij,bhj->bhi', S, r_t)

    seq = tuple(jnp.moveaxis(t, 1, 0) for t in (r, w, k, v, a_vec, b_vec))
    S0 = jnp.zeros((Bsz, H, N, N), jnp.float32)
    _, ys = lax.scan(step, S0, seq)
    return jnp.moveaxis(ys, 0, 1)


def rwkv_conv_mixer(x, w_in, mu, w0, w2, a0, a2, g2, k_k, k_a, r_k, lnx_g, lnx_b,
                    conv_w, w_out, v_first, v_res):
    Bsz, T, _ = x.shape
    RW, H, N = RWKV_WIDTH, RWKV_HEADS, RWKV_HEAD
    f32 = jnp.float32
    proj = x @ w_in
    pr, pc = proj[..., :RWKV_COLS], proj[..., RWKV_COLS:]
    pr = pr + (token_shift(pr) - pr) * mu
    idx = [RW, 2 * RW, 3 * RW, 3 * RW + DECAY_LORA, 3 * RW + DECAY_LORA + AAA_LORA]
    r, k, v, dw, da, dg = jnp.split(pr, idx, axis=-1)
    w_log = -jax.nn.softplus(-(w0 + jnp.tanh(dw) @ w2).astype(f32)) - 0.5
    decay = jnp.exp(-jnp.exp(w_log))
    a = jax.nn.sigmoid(a0 + da @ a2)
    g = jax.nn.sigmoid(dg) @ g2
    if v_res is None:
        v_first = v
    else:
        v0, v1, v2 = v_res
        v = v + (v_first - v) * jax.nn.sigmoid(v0 + (v @ v1) @ v2)
    hd = lambda t: t.reshape(Bsz, T, H, N)
    kk = hd(k * k_k).astype(f32)
    kk = kk / jnp.maximum(jnp.sqrt(jnp.sum(kk * kk, axis=-1, keepdims=True)), 1e-12)
    k = k * (1 + (a - 1) * k_a)
    rh, kh, vh = hd(r).astype(f32), hd(k).astype(f32), hd(v).astype(f32)
    y = rwkv7_scan(rh, hd(decay), kh, vh, -kk, kk * hd(a).astype(f32))
    m = jnp.mean(y, axis=-1, keepdims=True)
    var = jnp.mean(jnp.square(y - m), axis=-1, keepdims=True)
    y = ((y - m) * lax.rsqrt(var + GN_EPS)).reshape(Bsz, T, RW) * lnx_g + lnx_b
    bonus = (jnp.sum(rh * kh * r_k, axis=-1, keepdims=True) * vh).reshape(Bsz, T, RW)
    y_rwkv = ((y + bonus) * g).astype(x.dtype)
    gb, gc, h = jnp.split(pc, 3, axis=-1)
    y_conv = gb * causal_dwconv(gc * h, conv_w)
    out = jnp.concatenate([y_rwkv, y_conv], axis=-1) @ w_out
    return out, v_first


def fox_mixer(x, w_in, b_f, w_out):
    Bsz, T, D = x.shape
    H, Dh = FOX_HEADS, FOX_HEAD
    proj = x @ w_in
    q, k, v, fl = jnp.split(proj, [D, 2 * D, 3 * D], axis=-1)
    q = q.reshape(Bsz, T, H, Dh)
    k = k.reshape(Bsz, T, H, Dh)
    v = v.reshape(Bsz, T, H, Dh)
    log_f = jax.nn.log_sigmoid((fl + b_f).astype(jnp.float32))
    c = jnp.moveaxis(jnp.cumsum(log_f, axis=1), 1, 2)
    scale = FOX_HEAD ** -0.5
    outs = []
    for start in range(0, T, Q_BLOCK):
        end = start + Q_BLOCK
        logits = jnp.einsum('bqhd,bkhd->bhqk', q[:, start:end], k[:, :end]).astype(jnp.float32) * scale
        logits = logits + (c[:, :, start:end, None] - c[:, :, None, :end])
        mask = jnp.arange(start, end)[:, None] >= jnp.arange(end)[None, :]
        logits = jnp.where(mask, logits, -jnp.inf)
        p = jax.nn.softmax(logits, axis=-1).astype(v.dtype)
        outs.append(jnp.einsum('bhqk,bkhd->bqhd', p, v[:, :end]))
    o = jnp.concatenate(outs, axis=1).reshape(Bsz, T, D)
    return o @ w_out


def conv_ffn(x, w_up, conv_w, conv_b, w_down):
    u = causal_dwconv(x @ w_up, conv_w, conv_b)
    gate, val = jnp.split(u, 2, axis=-1)
    return (jax.nn.silu(gate) * val) @ w_down


def setup_inputs(seed: int = 0) -> dict:
    key = jax.random.key(seed)
    ks = jax.random.split(key, 32)
    f32 = jnp.float32
    nrm = lambda k, shape, s: jax.random.normal(k, shape, f32) * s
    D, RW, CW = D_MODEL, RWKV_WIDTH, CONV_WIDTH
    ratio = jnp.linspace(0.0, 1.0, RW, dtype=f32)
    return {
        'x': nrm(ks[0], (BATCH, SEQ, D), 1.0),
        'ln_g': 1.0 + nrm(ks[1], (DEPTH, 2, D), 0.02),
        'ln_b': nrm(ks[2], (DEPTH, 2, D), 0.02),
        'ev_w_in': nrm(ks[3], (N_EVEN, D, EVEN_COLS), D ** -0.5),
        'ev_mu': jax.random.uniform(ks[4], (N_EVEN, RWKV_COLS), f32),
        'ev_w0': -6.5 + 5.0 * ratio ** 0.85 + nrm(ks[5], (N_EVEN, RW), 0.1),
        'ev_w2': nrm(ks[6], (N_EVEN, DECAY_LORA, RW), 0.1 * DECAY_LORA ** -0.5),
        'ev_a0': nrm(ks[7], (N_EVEN, RW), 0.1),
        'ev_a2': nrm(ks[8], (N_EVEN, AAA_LORA, RW), 0.5 * AAA_LORA ** -0.5),
        'ev_g2': nrm(ks[9], (N_EVEN, GATE_LORA, RW), GATE_LORA ** -0.5),
        'ev_k_k': 0.85 + nrm(ks[10], (N_EVEN, RW), 0.05),
        'ev_k_a': 1.0 + nrm(ks[11], (N_EVEN, RW), 0.05),
        'ev_r_k': -0.04 + nrm(ks[12], (N_EVEN, RWKV_HEADS, RWKV_HEAD), 0.05),
        'ev_lnx_g': 1.0 + nrm(ks[13], (N_EVEN, RW), 0.02),
        'ev_lnx_b': nrm(ks[14], (N_EVEN, RW), 0.02),
        'ev_v0': 1.0 + nrm(ks[15], (N_EVEN - 1, RW), 0.1),
        'ev_v1': nrm(ks[16], (N_EVEN - 1, RW, MV_LORA), RW ** -0.5),
        'ev_v2': nrm(ks[17], (N_EVEN - 1, MV_LORA, RW), MV_LORA ** -0.5),
        'ev_conv_w': nrm(ks[18], (N_EVEN, SHORT_CONV, CW), SHORT_CONV ** -0.5),
        'ev_w_out': nrm(ks[19], (N_EVEN, D, D), DN_BETA * D ** -0.5),
        'od_w_in': nrm(ks[20], (N_ODD, D, ODD_COLS), D ** -0.5),
        'od_b_f': jnp.linspace(1.0, 6.0, FOX_HEADS, dtype=f32) + nrm(ks[21], (N_ODD, FOX_HEADS), 0.1),
        'od_w_out': nrm(ks[22], (N_ODD, D, D), DN_BETA * D ** -0.5),
        'ff_w_up': nrm(ks[23], (DEPTH, D, 2 * D_FF), D ** -0.5),
        'ff_conv_w': nrm(ks[24], (DEPTH, FFN_CONV, 2 * D_FF), FFN_CONV ** -0.5),
        'ff_conv_b': nrm(ks[25], (DEPTH, 2 * D_FF), 0.02),
        'ff_w_down': nrm(ks[26], (DEPTH, D_FF, D), DN_BETA * D_FF ** -0.5),
    }


def reference(x, ln_g, ln_b, ev_w_in, ev_mu, ev_w0, ev_w2, ev_a0, ev_a2, ev_g2, ev_k_k, ev_k_a,
              ev_r_k, ev_lnx_g, ev_lnx_b, ev_v0, ev_v1, ev_v2, ev_conv_w, ev_w_out,
              od_w_in, od_b_f, od_w_out, ff_w_up, ff_conv_w, ff_conv_b, ff_w_down):
    v_first = None
    for i in range(DEPTH):
        if i % 2 == 0:
            e = i // 2
            v_res = (ev_v0[e - 1], ev_v1[e - 1], ev_v2[e - 1]) if e > 0 else None
            mix, v_first = rwkv_conv_mixer(x, ev_w_in[e], ev_mu[e], ev_w0[e], ev_w2[e], ev_a0[e],
                                           ev_a2[e], ev_g2[e], ev_k_k[e], ev_k_a[e], ev_r_k[e],
                                           ev_lnx_g[e], ev_lnx_b[e], ev_conv_w[e], ev_w_out[e],
                                           v_first, v_res)
        else:
            o = i // 2
            mix = fox_mixer(x, od_w_in[o], od_b_f[o], od_w_out[o])
        x = layer_norm(DN_ALPHA * x + mix, ln_g[i, 0], ln_b[i, 0])
        ffn = conv_ffn(x, ff_w_up[i], ff_conv_w[i], ff_conv_b[i], ff_w_down[i])
        x = layer_norm(DN_ALPHA * x + ffn, ln_g[i, 1], ln_b[i, 1])
    return x
```

```python
from contextlib import ExitStack
import numpy as np
import concourse.bass as bass
import concourse.mybir as mybir
from concourse.bass_utils import run_bass_kernel_spmd

F32 = mybir.dt.float32
BF16 = mybir.dt.bfloat16
AF = mybir.ActivationFunctionType
ALU = mybir.AluOpType

D = 2048
DFF = 5632
NFC = DFF // 128
DEPTH = 4
ALPHA = float((2 * DEPTH) ** 0.25)
LN_EPS = 1e-5
NDMA = 40
ENGS = ("tensor", "vector", "scalar", "gpsimd", "sync")
WQ = ["sync"]
CQ = ["gpsimd"]
WDT = [F32]


class Buf:
    __slots__ = ("w", "r")

    def __init__(self):
        self.w = None
        self.r = {}


class Tl:
    def __init__(self, t):
        self.t = t
        self.b = Buf()

    def __getitem__(self, k):
        return self.t[k]


class Prog:
    def __init__(self, nc):
        self.nc = nc
        self.ops = {e: [] for e in ENGS}
        self.cnt = {}
        self.seen = {e: {} for e in ENGS}
        self.rr = 0
        self.nonce = 0
        self.dram = {}

    def dbuf(self, key):
        if key not in self.dram:
            self.dram[key] = Buf()
        return self.dram[key]

    def _deps(self, eng, reads, writes, skip_self=False):
        need = {}

        def add(dep):
            if dep is None:
                return
            k, c = dep
            if k == eng and (eng == "tensor" or skip_self):
                return
            if need.get(k, 0) < c:
                need[k] = c

        for b in reads:
            add(b.w)
        for b in writes:
            add(b.w)
            for k, c in b.r.items():
                add((k, c))
        out = []
        for k, c in need.items():
            if self.seen[eng].get(k, 0) < c:
                self.seen[eng][k] = c
                out.append((k, c))
        return out

    @staticmethod
    def _bufs(xs):
        return [x.b if isinstance(x, Tl) else x for x in xs]

    def op(self, eng, fn, reads=(), writes=(), skip_self=False):
        reads = self._bufs(reads)
        writes = self._bufs(writes)
        waits = self._deps(eng, reads, writes, skip_self)
        c = self.cnt.get(eng, 0) + 1
        self.cnt[eng] = c
        self.ops[eng].append((waits, fn, (eng, 1)))
        for b in reads:
            b.r[eng] = c
        for b in writes:
            b.w = (eng, c)
            b.r = {}

    def dma(self, q, fn, reads=(), writes=()):
        reads = self._bufs(reads)
        writes = self._bufs(writes)
        if q == "gpsimd":
            k = "once%d" % self.nonce
            self.nonce += 1
        else:
            k = "dma%d" % self.rr
            self.rr = (self.rr + 1) % NDMA
        waits = self._deps(q, reads, writes)
        prev = self.cnt.get(k, 0)
        if prev and self.seen[q].get(k, 0) < prev:
            self.seen[q][k] = prev
            waits.append((k, prev))
        c = prev + 16
        self.cnt[k] = c
        self.ops[q].append((waits, fn, (k, 16)))
        for b in reads:
            b.r[k] = c
        for b in writes:
            b.w = (k, c)
            b.r = {}

    def barrier(self):
        snap = dict(self.cnt)
        for e in ENGS:
            waits = []
            for k, c in snap.items():
                if k != e and self.seen[e].get(k, 0) < c:
                    self.seen[e][k] = c
                    waits.append((k, c))
            if waits:
                self.ops[e].append((waits, None, None))

    def emit(self):
        nc = self.nc
        keys = list(ENGS) + ["dma%d" % i for i in range(NDMA)] + ["once%d" % i for i in range(self.nonce)]
        with ExitStack() as es:
            sems = {k: es.enter_context(nc.semaphore("s_" + k)) for k in keys}
            block = es.enter_context(nc.Block())

            def run(eng, e):
                for waits, fn, inc in self.ops[eng]:
                    for k, c in waits:
                        e.wait_ge(sems[k], c)
                    if fn is not None:
                        ins = fn(e)
                        ins.then_inc(sems[inc[0]], inc[1])
                if eng == "sync":
                    for k, c in self.cnt.items():
                        if (k.startswith("dma") or k.startswith("once")) and self.seen[eng].get(k, 0) < c:
                            e.wait_ge(sems[k], c)

            @block.tensor
            def _(e):
                run("tensor", e)

            @block.vector
            def _(e):
                run("vector", e)

            @block.scalar
            def _(e):
                run("scalar", e)

            @block.gpsimd
            def _(e):
                run("gpsimd", e)

            @block.sync
            def _(e):
                run("sync", e)


class Ctx:
    pass


_UID = [0]


def uq(prefix):
    _UID[0] += 1
    return "%s%d_" % (prefix, _UID[0])


def load_xT(P, C, Xin, xkey, g, xs, xT, pst):
    ident = C.ident
    for tt in range(4):
        r0 = g * 512 + tt * 128
        pass
    i = 0
    nx = len(xs)
    for tt in range(4):
        r0 = g * 512 + tt * 128
        P.dma("sync", lambda e, tt=tt, r0=r0: e.dma_start(out=xs[tt % nx][:, :], in_=Xin[r0:r0 + 128, :]),
              reads=[P.dbuf((xkey, g))], writes=[xs[tt % nx]])
        for k4 in range(4):
            ps = pst[i % len(pst)]
            i += 1
            for j in range(4):
                kc = k4 * 4 + j
                P.op("tensor", lambda e, ps=ps, tt=tt, kc=kc, j=j: e.transpose(
                    ps[:, j * 128:(j + 1) * 128], xs[tt % nx][:, kc * 128:(kc + 1) * 128], ident[:, :]),
                    reads=[xs[tt % nx], ident], writes=[ps])
            eng = "scalar" if (i % 2) else "vector"
            if eng == "scalar":
                P.op("scalar", lambda e, ps=ps, tt=tt, k4=k4: e.activation(
                    out=xT[:, k4 * 4:(k4 + 1) * 4, tt * 128:(tt + 1) * 128],
                    in_=ps[:, :].rearrange("p (a b) -> p a b", a=4), func=AF.Copy),
                    reads=[ps], writes=[xT])
            else:
                P.op("vector", lambda e, ps=ps, tt=tt, k4=k4: e.tensor_copy(
                    out=xT[:, k4 * 4:(k4 + 1) * 4, tt * 128:(tt + 1) * 128],
                    in_=ps[:, :].rearrange("p (a b) -> p a b", a=4)),
                    reads=[ps], writes=[xT])


def resid_ln(P, C, xs_tt, lnG, lnB, sm):
    junk, mv, rs, nmr = sm
    AXX = mybir.AxisListType.X
    P.op("vector", lambda e: e.reduce_sum(out=mv[:, 0:1], in_=xs_tt[:, :], axis=AXX), reads=[xs_tt], writes=[mv])
    P.op("gpsimd", lambda e: e.tensor_tensor(out=junk[:, :], in0=xs_tt[:, :], in1=xs_tt[:, :], op=ALU.mult),
         reads=[xs_tt], writes=[junk])
    P.op("vector", lambda e: e.reduce_sum(out=mv[:, 1:2], in_=junk[:, :], axis=AXX), reads=[junk], writes=[mv])
    P.op("vector", lambda e: e.tensor_scalar(out=mv[:, 0:1], in0=mv[:, 0:1], scalar1=1.0 / D, scalar2=None,
                                             op0=ALU.mult), reads=[mv], writes=[mv])
    P.op("vector", lambda e: e.tensor_tensor(out=rs[:, :], in0=mv[:, 0:1], in1=mv[:, 0:1], op=ALU.mult),
         reads=[mv], writes=[rs])
    P.op("vector", lambda e: e.scalar_tensor_tensor(out=rs[:, :], in0=mv[:, 1:2], scalar=1.0 / D, in1=rs[:, :],
                                                    op0=ALU.mult, op1=ALU.subtract), reads=[mv, rs], writes=[rs])
    P.op("vector", lambda e: e.tensor_scalar(out=rs[:, :], in0=rs[:, :], scalar1=LN_EPS, scalar2=None,
                                             op0=ALU.add), reads=[rs], writes=[rs])
    P.op("scalar", lambda e: e.activation(out=rs[:, :], in_=rs[:, :], func=AF.Sqrt), reads=[rs], writes=[rs])
    P.op("vector", lambda e: e.reciprocal(out=rs[:, :], in_=rs[:, :]), reads=[rs], writes=[rs])
    P.op("vector", lambda e: e.tensor_scalar(out=nmr[:, :], in0=mv[:, 0:1], scalar1=rs[:, 0:1], scalar2=-1.0,
                                             op0=ALU.mult, op1=ALU.mult), reads=[mv, rs], writes=[nmr])
    P.op("scalar", lambda e: e.activation(out=xs_tt[:, :], in_=xs_tt[:, :], func=AF.Identity,
                                          bias=nmr[:, 0:1], scale=rs[:, 0:1]),
         reads=[xs_tt, rs, nmr], writes=[xs_tt])
    P.op("vector", lambda e: e.tensor_tensor(out=xs_tt[:, :], in0=xs_tt[:, :], in1=lnG[:, :], op=ALU.mult),
         reads=[xs_tt, lnG], writes=[xs_tt])
    P.op("gpsimd", lambda e: e.tensor_tensor(out=xs_tt[:, :], in0=xs_tt[:, :], in1=lnB[:, :], op=ALU.add),
         reads=[xs_tt, lnB], writes=[xs_tt])


def ffn_phase(P, C, l, Xin, xin_key, Xout, xout_key, T):
    nc = P.nc
    _pfx = uq("t")
    w_up = C.ff_w_up[l].rearrange("(kc p) n -> p kc n", p=128)
    w_dn = C.ff_w_down[l].rearrange("(c p) n -> p c n", p=128)
    with ExitStack() as es:
        def A(name, shape, dt):
            return Tl(es.enter_context(nc.sbuf_tensor(_pfx + name, shape, dt)))

        def PS(name):
            return Tl(es.enter_context(nc.psum_tensor(_pfx + "p" + name, [128, 512], F32)))

        xs = [A("xs%d" % i, [128, D], F32) for i in range(4)]
        xT = A("xT", [128, 16, 512], BF16)
        wg = [A("wg%d" % i, [128, 16, 256], BF16) for i in range(2)]
        wv = [A("wv%d" % i, [128, 16, 256], BF16) for i in range(2)]
        hT = A("hT", [128, NFC, 512], BF16)
        wd = [A("wd%d" % i, [128, 4, 512], BF16) for i in range(3)]
        ug = [A("ug%d" % i, [128, 514], F32) for i in range(2)]
        uv = [A("uv%d" % i, [128, 514], F32) for i in range(2)]
        ag = [A("ag%d" % i, [128, 512], F32) for i in range(2)]
        av = [A("av%d" % i, [128, 512], F32) for i in range(2)]
        carry = A("carry", [128, 2 * NFC, 2], F32)
        cwb = A("cwb", [128, 4, 2 * NFC], F32)
        lnG = A("lnG", [128, D], F32)
        lnB = A("lnB", [128, D], F32)
        sm = (A("junk", [128, D], F32), A("mv", [128, 2], F32), A("rs", [128, 1], F32), A("nmr", [128, 1], F32))
        pg = [PS("g%d" % i) for i in range(2)]
        pv = [PS("v%d" % i) for i in range(2)]
        pd = [PS("d%d" % i) for i in range(4)]

        P.dma("sync", lambda e: e.dma_start(out=cwb[:, :, :], in_=C.ff_cwb[l]), writes=[cwb])
        P.dma("sync", lambda e: e.dma_start(out=lnG[:, :], in_=C.ln_g[l, 1]), writes=[lnG])
        P.dma("sync", lambda e: e.dma_start(out=lnB[:, :], in_=C.ln_b[l, 1]), writes=[lnB])
        P.op("gpsimd", lambda e: e.memset(carry[:, :, :], 0.0), writes=[carry])

        wcount = 0
        dcount = 0
        for g in range(T // 512):
            load_xT(P, C, Xin, xin_key, g, xs, xT, pg + pv)
            for c in range(NFC):
                cq, j = divmod(c, 2)
                if j == 0:
                    wb = wcount % 2
                    wcount += 1
                    for k4 in range(2):
                        P.dma(WQ[0], lambda e, wb=wb, k4=k4, cq=cq: e.dma_start(
                            out=wg[wb][:, k4 * 8:(k4 + 1) * 8, :],
                            in_=w_up[:, k4 * 8:(k4 + 1) * 8, cq * 256:(cq + 1) * 256]), writes=[wg[wb]])
                        P.dma(WQ[0], lambda e, wb=wb, k4=k4, cq=cq: e.dma_start(
                            out=wv[wb][:, k4 * 8:(k4 + 1) * 8, :],
                            in_=w_up[:, k4 * 8:(k4 + 1) * 8, DFF + cq * 256:DFF + (cq + 1) * 256]), writes=[wv[wb]])
                s = c % 2
                for kc in range(16):
                    P.op("tensor", lambda e, s=s, wb=wb, kc=kc, j=j: e.matmul(
                        pg[s][:, :], lhsT=wg[wb][:, kc, j * 128:(j + 1) * 128], rhs=xT[:, kc, :],
                        start=(kc == 0), stop=(kc == 15)), reads=[wg[wb], xT], writes=[pg[s]])
                for kc in range(16):
                    P.op("tensor", lambda e, s=s, wb=wb, kc=kc, j=j: e.matmul(
                        pv[s][:, :], lhsT=wv[wb][:, kc, j * 128:(j + 1) * 128], rhs=xT[:, kc, :],
                        start=(kc == 0), stop=(kc == 15)), reads=[wv[wb], xT], writes=[pv[s]])
                P.op("scalar", lambda e, s=s: e.activation(out=ug[s][:, 2:514], in_=pg[s][:, :], func=AF.Copy),
                     reads=[pg[s]], writes=[ug[s]])
                P.op("scalar", lambda e, s=s: e.activation(out=uv[s][:, 2:514], in_=pv[s][:, :], func=AF.Copy),
                     reads=[pv[s]], writes=[uv[s]])
                P.op("gpsimd", lambda e, s=s, c=c: e.tensor_copy(out=ug[s][:, 0:2], in_=carry[:, c, :]),
                     reads=[carry], writes=[ug[s]])
                P.op("gpsimd", lambda e, s=s, c=c: e.tensor_copy(out=uv[s][:, 0:2], in_=carry[:, NFC + c, :]),
                     reads=[carry], writes=[uv[s]])
                P.op("gpsimd", lambda e, s=s, c=c: e.tensor_copy(out=carry[:, c, :], in_=ug[s][:, 512:514]),
                     reads=[ug[s]], writes=[carry])
                P.op("gpsimd", lambda e, s=s, c=c: e.tensor_copy(out=carry[:, NFC + c, :], in_=uv[s][:, 512:514]),
                     reads=[uv[s]], writes=[carry])
                for (u, a, pp, cc) in ((ug[s], ag[s], pg[s], c), (uv[s], av[s], pv[s], NFC + c)):
                    P.op("scalar", lambda e, a=a, pp=pp, cc=cc: e.activation(
                        out=a[:, :], in_=pp[:, :], func=AF.Identity,
                        bias=cwb[:, 3, cc:cc + 1], scale=cwb[:, 2, cc:cc + 1]), reads=[pp, cwb], writes=[a])
                    P.op("vector", lambda e, u=u, a=a, cc=cc: e.scalar_tensor_tensor(
                        out=a[:, :], in0=u[:, 1:513], scalar=cwb[:, 1, cc:cc + 1], in1=a[:, :],
                        op0=ALU.mult, op1=ALU.add), reads=[u, cwb, a], writes=[a])
                    P.op("vector", lambda e, u=u, a=a, cc=cc: e.scalar_tensor_tensor(
                        out=a[:, :], in0=u[:, 0:512], scalar=cwb[:, 0, cc:cc + 1], in1=a[:, :],
                        op0=ALU.mult, op1=ALU.add), reads=[u, cwb, a], writes=[a])
                P.op("scalar", lambda e, s=s: e.activation(out=ag[s][:, :], in_=ag[s][:, :], func=AF.Silu),
                     reads=[ag[s]], writes=[ag[s]])
                P.op("vector", lambda e, s=s, c=c: e.tensor_tensor(
                    out=hT[:, c, :], in0=ag[s][:, :], in1=av[s][:, :], op=ALU.mult),
                    reads=[ag[s], av[s]], writes=[hT])
            if C.dbg and g == 0:
                P.dma("sync", lambda e: e.dma_start(out=C.dbg["xT"], in_=xT[:, :, :]), reads=[xT])
                P.dma("sync", lambda e: e.dma_start(out=C.dbg["hT"], in_=hT[:, :, :]), reads=[hT])
            for fg in range(4):
                for c4 in range(NFC // 4):
                    db = dcount % 3
                    dcount += 1
                    P.dma(WQ[0], lambda e, db=db, c4=c4, fg=fg: e.dma_start(
                        out=wd[db][:, :, :], in_=w_dn[:, c4 * 4:(c4 + 1) * 4, fg * 512:(fg + 1) * 512]),
                        writes=[wd[db]])
                    for i in range(4):
                        c = c4 * 4 + i
                        for tt in range(4):
                            P.op("tensor", lambda e, db=db, i=i, c=c, tt=tt: e.matmul(
                                pd[tt][:, :], lhsT=hT[:, c, tt * 128:(tt + 1) * 128], rhs=wd[db][:, i, :],
                                start=(c == 0), stop=(c == NFC - 1)), reads=[hT, wd[db]], writes=[pd[tt]])
                for tt in range(4):
                    P.op("vector", lambda e, tt=tt, fg=fg: e.scalar_tensor_tensor(
                        out=xs[tt][:, fg * 512:(fg + 1) * 512], in0=xs[tt][:, fg * 512:(fg + 1) * 512],
                        scalar=ALPHA, in1=pd[tt][:, :], op0=ALU.mult, op1=ALU.add),
                        reads=[xs[tt], pd[tt]], writes=[xs[tt]])
            if C.dbg and g == 0:
                P.dma("sync", lambda e: e.dma_start(out=C.dbg["z"], in_=xs[0][:, :]), reads=[xs[0]])
            for tt in range(4):
                resid_ln(P, C, xs[tt], lnG, lnB, sm)
                r0 = g * 512 + tt * 128
                P.dma("sync", lambda e, tt=tt, r0=r0: e.dma_start(out=Xout[r0:r0 + 128, :], in_=xs[tt][:, :]),
                      reads=[xs[tt]], writes=[P.dbuf((xout_key, g))])
    P.barrier()


def precast_phase(P, C, pairs):
    nc = P.nc
    _pfx = uq("t")
    CH = 8192
    with ExitStack() as es:
        stg = [Tl(es.enter_context(nc.sbuf_tensor(_pfx + "stg%d" % i, [128, CH], F32))) for i in range(2)]
        stb = [Tl(es.enter_context(nc.sbuf_tensor(_pfx + "stb%d" % i, [128, CH], BF16))) for i in range(2)]
        i = 0
        for src, dst in pairs:
            R, Cc = src.shape
            per = (R // 128) * Cc
            sv = src.rearrange("(p r) c -> p (r c)", p=128)
            dv = dst.rearrange("(p r) c -> p (r c)", p=128)
            for o in range(0, per, CH):
                n = min(CH, per - o)
                b = i % 2
                P.dma("sync", lambda e, b=b, o=o, n=n, sv=sv: e.dma_start(out=stg[b][:, :n], in_=sv[:, o:o + n]),
                      writes=[stg[b]])
                if i % 2:
                    P.op("scalar", lambda e, b=b, n=n: e.activation(out=stb[b][:, :n], in_=stg[b][:, :n], func=AF.Copy),
                         reads=[stg[b]], writes=[stb[b]])
                else:
                    P.op("vector", lambda e, b=b, n=n: e.tensor_copy(out=stb[b][:, :n], in_=stg[b][:, :n]),
                         reads=[stg[b]], writes=[stb[b]])
                P.dma("scalar", lambda e, b=b, o=o, n=n, dv=dv: e.dma_start(out=dv[:, o:o + n], in_=stb[b][:, :n]),
                      reads=[stb[b]])
                i += 1
    P.barrier()


class PSPool:
    def __init__(self, nc, es, prefix, n=8):
        self.t = [Tl(es.enter_context(nc.psum_tensor("%s_ps%d" % (prefix, i), [128, 512], F32))) for i in range(n)]
        self.i = 0

    def next(self):
        t = self.t[self.i % len(self.t)]
        self.i += 1
        return t


def outproj_phase(P, C, srcT, src_key, w_out, l, Xin, xin_key, Xout, xout_key, T):
    nc = P.nc
    _pfx = uq("t")
    wv_ = w_out.rearrange("(c p) n -> p c n", p=128)
    sv = srcT.rearrange("(c p) t -> p c t", p=128)
    with ExitStack() as es:
        def A(name, shape, dt):
            return Tl(es.enter_context(nc.sbuf_tensor(_pfx + name, shape, dt)))
        xs = [A("xs%d" % i, [128, D], F32) for i in range(4)]
        hT = A("hT", [128, 16, 512], BF16)
        wd = [A("wd%d" % i, [128, 4, 512], BF16) for i in range(3)]
        lnG = A("lnG", [128, D], F32)
        lnB = A("lnB", [128, D], F32)
        sm = (A("junk", [128, D], F32), A("mv", [128, 2], F32), A("rs", [128, 1], F32), A("nmr", [128, 1], F32))
        pp = PSPool(nc, es, _pfx, 8)
        P.dma("sync", lambda e: e.dma_start(out=lnG[:, :], in_=C.ln_g[l, 0]), writes=[lnG])
        P.dma("sync", lambda e: e.dma_start(out=lnB[:, :], in_=C.ln_b[l, 0]), writes=[lnB])
        dcount = 0
        for g in range(T // 512):
            for tt in range(4):
                r0 = g * 512 + tt * 128
                P.dma("sync", lambda e, tt=tt, r0=r0: e.dma_start(out=xs[tt][:, :], in_=Xin[r0:r0 + 128, :]),
                      reads=[P.dbuf((xin_key, g))], writes=[xs[tt]])
            for k4 in range(4):
                P.dma("sync", lambda e, k4=k4, g=g: e.dma_start(
                    out=hT[:, k4 * 4:(k4 + 1) * 4, :], in_=sv[:, k4 * 4:(k4 + 1) * 4, g * 512:(g + 1) * 512]),
                    reads=[P.dbuf((src_key, g))], writes=[hT])
            for fg in range(4):
                pd = [pp.next() for _ in range(4)]
                for c4 in range(4):
                    db = dcount % 3
                    dcount += 1
                    P.dma(WQ[0], lambda e, db=db, c4=c4, fg=fg: e.dma_start(
                        out=wd[db][:, :, :], in_=wv_[:, c4 * 4:(c4 + 1) * 4, fg * 512:(fg + 1) * 512]),
                        writes=[wd[db]])
                    for i in range(4):
                        c = c4 * 4 + i
                        for tt in range(4):
                            P.op("tensor", lambda e, db=db, i=i, c=c, tt=tt, pd=pd: e.matmul(
                                pd[tt][:, :], lhsT=hT[:, c, tt * 128:(tt + 1) * 128], rhs=wd[db][:, i, :],
                                start=(c == 0), stop=(c == 15)), reads=[hT, wd[db]], writes=[pd[tt]])
                for tt in range(4):
                    P.op("vector", lambda e, tt=tt, fg=fg, pd=pd: e.scalar_tensor_tensor(
                        out=xs[tt][:, fg * 512:(fg + 1) * 512], in0=xs[tt][:, fg * 512:(fg + 1) * 512],
                        scalar=ALPHA, in1=pd[tt][:, :], op0=ALU.mult, op1=ALU.add),
                        reads=[xs[tt], pd[tt]], writes=[xs[tt]])
            for tt in range(4):
                resid_ln(P, C, xs[tt], lnG, lnB, sm)
                r0 = g * 512 + tt * 128
                P.dma("sync", lambda e, tt=tt, r0=r0: e.dma_start(out=Xout[r0:r0 + 128, :], in_=xs[tt][:, :]),
                      reads=[xs[tt]], writes=[P.dbuf((xout_key, g))])
    P.barrier()


RW = 1024
C_R, C_K, C_V, C_DW, C_DA, C_DG = 0, 1024, 2048, 3072, 3136, 3200
C_GB, C_GC, C_H = 3360, 4384, 5408
PP_MUR, PP_MUK, PP_MUV, PP_W0, PP_A0, PP_KK, PP_KA, PP_LG, PP_LB, PP_V0 = [16 * i for i in range(10)]
PP_MUDW, PP_MUDA, PP_MUDG = 160, 161, 162
NPP = 168
GN_EPS = 64e-5
NEG_EHALF = -float(np.exp(-0.5))


def rwkv_prep_phase(P, C, e, l, Xin, xin_key, T):
    nc = P.nc
    _pfx = uq("t")
    NCH = T // 64
    w_in = C.ev_w_in[e].rearrange("(kc p) n -> p kc n", p=128)
    S = C.scr
    with ExitStack() as es:
        def A(name, shape, dt=F32):
            return Tl(es.enter_context(nc.sbuf_tensor(_pfx + name, shape, dt)))
        xs = [A("xs%d" % i, [128, D]) for i in range(1)]
        xT = A("xT", [128, 16, 512], BF16)
        wl = A("wl", [128, 16, 288], BF16)
        wq = [[A("wq%d_%d" % (q, i), [128, 16, 256], BF16) for i in range(2)] for q in range(3)]
        pp_ = A("pp", [64, NPP])
        w2b = A("w2b", [64, 1024], BF16)
        a2b = A("a2b", [64, 1024], BF16)
        g2b = A("g2b", [64, 3, 1024], BF16)
        rkm = A("rkm", [64, 16, 64])
        ones64 = A("ones64", [64, 64])
        rst = A("rst", [64, 512])
        cw = A("cw", [128, 3, 8])
        if e > 0:
            v1s = A("v1s", [64, 16, 32])
            v2s = A("v2s", [32, 1024])
            vv1s = A("vv1s", [32, 512])
        carry = A("carry", [64, 56])
        ccarry = A("ccarry", [128, 8, 2])
        ub = [A("ub%d" % i, [64, 513]) for i in range(2)]
        dd = [A("dd%d" % i, [64, 512]) for i in range(2)]
        tdw = A("tdw", [64, 512], BF16)
        tda = A("tda", [64, 512], BF16)
        sdg = A("sdg", [64, 3, 512], BF16)
        vall = A("vall", [64, 16, 512])
        names = ["rm", "km", "lw", "ag", "gt", "kk", "sq", "rn", "kkn", "kp", "bv",
                 "cl", "cle", "e1", "e2", "bt", "kt"]
        W = {n: A(n, [64, 512]) for n in names}
        W["vf"] = W["kk"]
        W["sv"] = W["kkn"]
        W["rkp"] = W["sq"]
        W["t1"] = W["rn"]
        W["bon"] = W["rn"]
        W["e3"] = W["cle"]
        lmix = W["sq"]
        ar = A("ar", [64, 8, 2, 64])
        wcs = A("wcs", [64, 8])
        tk = [A("tk%d" % i, [64, 8, 64]) for i in range(3)]
        hs_ = A("hs", [128, 512])
        uc = A("uc", [128, 514])
        acc = A("acc", [128, 512])
        ycb = A("ycb", [128, 512], BF16)
        pp = PSPool(nc, es, _pfx, 8)
        ident = C.ident

        P.dma("sync", lambda e_: e_.dma_start(out=pp_[:, :], in_=C.ev_pp[e]), writes=[pp_])
        P.dma(CQ[0], lambda e_: e_.dma_start(out=w2b[:, :], in_=C.ev_w2[e]), writes=[w2b])
        P.dma(CQ[0], lambda e_: e_.dma_start(out=a2b[:, :], in_=C.ev_a2[e]), writes=[a2b])
        P.dma(CQ[0], lambda e_: e_.dma_start(out=g2b[:, :, :], in_=C.ev_g2[e]), writes=[g2b])
        P.dma("sync", lambda e_: e_.dma_start(out=rkm[:, :, :], in_=C.ev_rkmat[e]), writes=[rkm])
        P.dma("sync", lambda e_: e_.dma_start(out=ones64[:, :], in_=C.c_ones64), writes=[ones64])
        P.dma("sync", lambda e_: e_.dma_start(out=rst[:, :], in_=C.c_rst), writes=[rst])
        P.dma("sync", lambda e_: e_.dma_start(out=cw[:, :, :], in_=C.ev_cw[e]), writes=[cw])
        if e > 0:
            P.dma("sync", lambda e_: e_.dma_start(out=v1s[:, :, :], in_=C.ev_v1[e - 1]), writes=[v1s])
            P.dma("sync", lambda e_: e_.dma_start(out=v2s[:, :], in_=C.ev_v2[e - 1]), writes=[v2s])
        P.op("gpsimd", lambda e_: e_.memset(carry[:, :], 0.0), writes=[carry])
        P.op("gpsimd", lambda e_: e_.memset(ccarry[:, :, :], 0.0), writes=[ccarry])

        mixn = [0]

        def proj(dst_ps, wt, c0, M):
            for kc in range(16):
                P.op("tensor", lambda e_, kc=kc: e_.matmul(
                    dst_ps[:M, :], lhsT=wt[:, kc, c0:c0 + M], rhs=xT[:, kc, :],
                    start=(kc == 0), stop=(kc == 15)), reads=[wt, xT], writes=[dst_ps])

        def mix(ps, M, mucol, cidx, out_t, out_ap_fn):
            i = mixn[0] % 2
            mixn[0] += 1
            u, d = ub[i], dd[i]
            P.op("scalar", lambda e_: e_.activation(out=u[:M, 1:513], in_=ps[:M, :], func=AF.Copy),
                 reads=[ps], writes=[u])
            P.op("gpsimd", lambda e_: e_.tensor_copy(out=u[:M, 0:1], in_=carry[:M, cidx:cidx + 1]),
                 reads=[carry], writes=[u])
            P.op("gpsimd", lambda e_: e_.tensor_copy(out=carry[:M, cidx:cidx + 1], in_=u[:M, 512:513]),
                 reads=[u], writes=[carry])
            P.op("vector", lambda e_: e_.tensor_tensor(out=d[:M, :], in0=u[:M, 0:512], in1=u[:M, 1:513],
                                                       op=ALU.subtract), reads=[u], writes=[d])
            P.op("vector", lambda e_: e_.scalar_tensor_tensor(
                out=out_ap_fn(), in0=d[:M, :], scalar=pp_[:M, mucol:mucol + 1], in1=u[:M, 1:513],
                op0=ALU.mult, op1=ALU.add), reads=[d, u, pp_], writes=[out_t])

        def tt_(eng, out_t, a, b, op, reads):
            P.op(eng, lambda e_: e_.tensor_tensor(out=out_t[:, :], in0=a, in1=b, op=op), reads=reads, writes=[out_t])

        wcnt = 0
        for g in range(T // 512):
            c0g = g * 512
            load_xT(P, C, Xin, xin_key, g, xs, xT, [pp.next() for _ in range(4)])
            for k4 in range(4):
                P.dma(WQ[0], lambda e_, k4=k4: e_.dma_start(
                    out=wl[:, k4 * 4:(k4 + 1) * 4, :], in_=w_in[:, k4 * 4:(k4 + 1) * 4, C_DW:C_DW + 288]), writes=[wl])
            for ci, (c0, M, mucol) in enumerate(((0, 64, PP_MUDW), (64, 64, PP_MUDA), (128, 64, PP_MUDG),
                                                 (192, 64, PP_MUDG + 1), (256, 32, PP_MUDG + 2))):
                ps = pp.next()
                proj(ps, wl, c0, M)
                mix(ps, M, mucol, 48 + ci, lmix, lambda M=M: lmix[:M, :])
                if ci == 0:
                    P.op("scalar", lambda e_: e_.activation(out=tdw[:, :], in_=lmix[:, :], func=AF.Tanh),
                         reads=[lmix], writes=[tdw])
                elif ci == 1:
                    P.op("scalar", lambda e_: e_.activation(out=tda[:, :], in_=lmix[:, :], func=AF.Copy),
                         reads=[lmix], writes=[tda])
                else:
                    P.op("scalar", lambda e_, M=M, q=ci - 2: e_.activation(
                        out=sdg[:M, q, :], in_=lmix[:M, :], func=AF.Sigmoid), reads=[lmix], writes=[sdg])
            for h in range(16):
                hq, hh = divmod(h, 4)
                if hh == 0:
                    wb = wcnt % 2
                    wcnt += 1
                    for k4 in range(4):
                        P.dma(WQ[0], lambda e_, k4=k4, wb=wb, hq=hq: e_.dma_start(
                            out=wq[2][wb][:, k4 * 4:(k4 + 1) * 4, :],
                            in_=w_in[:, k4 * 4:(k4 + 1) * 4, C_V + hq * 256:C_V + (hq + 1) * 256]), writes=[wq[2][wb]])
                ps = pp.next()
                proj(ps, wq[2][wb], hh * 64, 64)
                mix(ps, 64, PP_MUV + h, 32 + h, vall, lambda h=h: vall[:, h, :])
            if e == 0:
                for h in range(16):
                    P.dma("sync", lambda e_, h=h, c0g=c0g: e_.dma_start(out=S["vfirst"][h, :, c0g:c0g + 512], in_=vall[:, h, :]),
                          reads=[vall], writes=[P.dbuf(("vfirst", g))])
            else:
                ps = pp.next()
                for h in range(16):
                    P.op("tensor", lambda e_, h=h: e_.matmul(ps[:32, :], lhsT=v1s[:, h, :], rhs=vall[:, h, :],
                                                            start=(h == 0), stop=(h == 15)),
                         reads=[v1s, vall], writes=[ps])
                P.op("scalar", lambda e_: e_.activation(out=vv1s[:, :], in_=ps[:32, :], func=AF.Copy),
                     reads=[ps], writes=[vv1s])
                for h in range(16):
                    ps2 = pp.next()
                    P.op("tensor", lambda e_, h=h, ps2=ps2: e_.matmul(
                        ps2[:64, :], lhsT=v2s[:, h * 64:(h + 1) * 64], rhs=vv1s[:, :], start=True, stop=True),
                        reads=[v2s, vv1s], writes=[ps2])
                    P.op("scalar", lambda e_, h=h, ps2=ps2: e_.activation(
                        out=W["sv"][:, :], in_=ps2[:64, :], func=AF.Sigmoid, bias=pp_[:, PP_V0 + h:PP_V0 + h + 1]),
                        reads=[ps2, pp_], writes=[W["sv"]])
                    P.dma("sync", lambda e_, h=h, c0g=c0g: e_.dma_start(out=W["vf"][:, :], in_=S["vfirst"][h, :, c0g:c0g + 512]),
                          reads=[P.dbuf(("vfirst", g))], writes=[W["vf"]])
                    tt_("vector", W["vf"], W["vf"][:, :], vall[:, h, :], ALU.subtract, [W["vf"], vall])
                    tt_("vector", W["vf"], W["vf"][:, :], W["sv"][:, :], ALU.mult, [W["vf"], W["sv"]])
                    P.op("vector", lambda e_, h=h: e_.tensor_tensor(
                        out=vall[:, h, :], in0=vall[:, h, :], in1=W["vf"][:, :], op=ALU.add),
                        reads=[vall, W["vf"]], writes=[vall])
            for h in range(16):
                hq, hh = divmod(h, 4)
                if hh == 0:
                    wb = wcnt % 2
                    wcnt += 1
                    for q, cbase in ((0, C_R), (1, C_K)):
                        for k4 in range(4):
                            P.dma(WQ[0], lambda e_, k4=k4, wb=wb, hq=hq, q=q, cbase=cbase: e_.dma_start(
                                out=wq[q][wb][:, k4 * 4:(k4 + 1) * 4, :],
                                in_=w_in[:, k4 * 4:(k4 + 1) * 4, cbase + hq * 256:cbase + (hq + 1) * 256]),
                                writes=[wq[q][wb]])
                hc = slice(h * 64, (h + 1) * 64)
                ps = pp.next()
                proj(ps, wq[0][wb], hh * 64, 64)
                mix(ps, 64, PP_MUR + h, h, W["rm"], lambda: W["rm"][:, :])
                ps = pp.next()
                proj(ps, wq[1][wb], hh * 64, 64)
                mix(ps, 64, PP_MUK + h, 16 + h, W["km"], lambda: W["km"][:, :])
                ps = pp.next()
                P.op("tensor", lambda e_, ps=ps, hc=hc: e_.matmul(ps[:64, :], lhsT=w2b[:, hc], rhs=tdw[:, :],
                                                                 start=True, stop=True), reads=[w2b, tdw], writes=[ps])
                P.op("scalar", lambda e_, ps=ps, h=h: e_.activation(
                    out=W["lw"][:, :], in_=ps[:64, :], func=AF.Sigmoid, bias=pp_[:, PP_W0 + h:PP_W0 + h + 1]),
                    reads=[ps, pp_], writes=[W["lw"]])
                P.op("vector", lambda e_: e_.tensor_scalar(out=W["lw"][:, :], in0=W["lw"][:, :], scalar1=NEG_EHALF,
                                                          scalar2=None, op0=ALU.mult), reads=[W["lw"]], writes=[W["lw"]])
                ps = pp.next()
                P.op("tensor", lambda e_, ps=ps, hc=hc: e_.matmul(ps[:64, :], lhsT=a2b[:, hc], rhs=tda[:, :],
                                                                 start=True, stop=True), reads=[a2b, tda], writes=[ps])
                P.op("scalar", lambda e_, ps=ps, h=h: e_.activation(
                    out=W["ag"][:, :], in_=ps[:64, :], func=AF.Sigmoid, bias=pp_[:, PP_A0 + h:PP_A0 + h + 1]),
                    reads=[ps, pp_], writes=[W["ag"]])
                ps = pp.next()
                for q, K in ((0, 64), (1, 64), (2, 32)):
                    P.op("tensor", lambda e_, ps=ps, hc=hc, q=q, K=K: e_.matmul(
                        ps[:64, :], lhsT=g2b[:K, q, hc], rhs=sdg[:K, q, :], start=(q == 0), stop=(q == 2)),
                        reads=[g2b, sdg], writes=[ps])
                P.op("scalar", lambda e_, ps=ps: e_.activation(out=W["gt"][:, :], in_=ps[:64, :], func=AF.Copy),
                     reads=[ps], writes=[W["gt"]])
                P.dma("sync", lambda e_, h=h, c0g=c0g: e_.dma_start(out=S["gT"][h, :, c0g:c0g + 512], in_=W["gt"][:, :]),
                      reads=[W["gt"]], writes=[P.dbuf(("gT", g))])
                P.op("vector", lambda e_, h=h: e_.tensor_scalar(
                    out=W["kk"][:, :], in0=W["km"][:, :], scalar1=pp_[:, PP_KK + h:PP_KK + h + 1], scalar2=None,
                    op0=ALU.mult), reads=[W["km"], pp_], writes=[W["kk"]])
                tt_("gpsimd", W["sq"], W["kk"][:, :], W["kk"][:, :], ALU.mult, [W["kk"]])
                ps = pp.next()
                P.op("tensor", lambda e_, ps=ps: e_.matmul(ps[:64, :], lhsT=ones64[:, :], rhs=W["sq"][:, :],
                                                          start=True, stop=True), reads=[ones64, W["sq"]], writes=[ps])
                P.op("vector", lambda e_, ps=ps: e_.tensor_scalar(out=W["rn"][:, :], in0=ps[:64, :], scalar1=1e-24,
                                                                 scalar2=None, op0=ALU.max), reads=[ps], writes=[W["rn"]])
                P.op("scalar", lambda e_: e_.activation(out=W["rn"][:, :], in_=W["rn"][:, :], func=AF.Sqrt),
                     reads=[W["rn"]], writes=[W["rn"]])
                P.op("vector", lambda e_: e_.reciprocal(out=W["rn"][:, :], in_=W["rn"][:, :]),
                     reads=[W["rn"]], writes=[W["rn"]])
                tt_("vector", W["kkn"], W["kk"][:, :], W["rn"][:, :], ALU.mult, [W["kk"], W["rn"]])
                P.op("vector", lambda e_, h=h: e_.tensor_scalar(
                    out=W["t1"][:, :], in0=W["ag"][:, :], scalar1=pp_[:, PP_KA + h:PP_KA + h + 1],
                    scalar2=pp_[:, PP_KA + h:PP_KA + h + 1], op0=ALU.mult, op1=ALU.subtract),
                    reads=[W["ag"], pp_], writes=[W["t1"]])
                P.op("vector", lambda e_: e_.scalar_tensor_tensor(
                    out=W["kp"][:, :], in0=W["t1"][:, :], scalar=1.0, in1=W["km"][:, :], op0=ALU.add, op1=ALU.mult),
                    reads=[W["t1"], W["km"]], writes=[W["kp"]])
                tt_("gpsimd", W["bv"], W["kkn"][:, :], W["ag"][:, :], ALU.mult, [W["kkn"], W["ag"]])
                tt_("gpsimd", W["rkp"], W["rm"][:, :], W["kp"][:, :], ALU.mult, [W["rm"], W["kp"]])
                ps = pp.next()
                P.op("tensor", lambda e_, ps=ps, h=h: e_.matmul(ps[:64, :], lhsT=rkm[:, h, :], rhs=W["rkp"][:, :],
                                                               start=True, stop=True), reads=[rkm, W["rkp"]], writes=[ps])
                P.op("vector", lambda e_, ps=ps, h=h: e_.tensor_tensor(
                    out=W["bon"][:, :], in0=ps[:64, :], in1=vall[:, h, :], op=ALU.mult),
                    reads=[ps, vall], writes=[W["bon"]])
                P.dma("sync", lambda e_, h=h, c0g=c0g: e_.dma_start(out=S["bonT"][h, :, c0g:c0g + 512], in_=W["bon"][:, :]),
                      reads=[W["bon"]], writes=[P.dbuf(("bonT", g))])
                P.op("vector", lambda e_: e_.tensor_tensor_scan(
                    out=W["cl"][:, :], data0=rst[:, :], data1=W["lw"][:, :], initial=0.0, op0=ALU.mult, op1=ALU.add),
                    reads=[rst, W["lw"]], writes=[W["cl"]])
                tt_("gpsimd", W["cle"], W["cl"][:, :], W["lw"][:, :], ALU.subtract, [W["cl"], W["lw"]])
                P.op("scalar", lambda e_: e_.activation(out=W["e1"][:, :], in_=W["cl"][:, :], func=AF.Exp),
                     reads=[W["cl"]], writes=[W["e1"]])
                P.op("scalar", lambda e_: e_.activation(out=W["e2"][:, :], in_=W["cl"][:, :], func=AF.Exp, scale=-1.0),
                     reads=[W["cl"]], writes=[W["e2"]])
                P.op("scalar", lambda e_: e_.activation(out=W["e3"][:, :], in_=W["cle"][:, :], func=AF.Exp),
                     reads=[W["cle"]], writes=[W["e3"]])
                P.op("vector", lambda e_: e_.scalar_tensor_tensor(
                    out=ar[:, :, 0, :], in0=W["kkn"][:, :].rearrange("p (c t) -> p c t", c=8), scalar=-1.0,
                    in1=W["e3"][:, :].rearrange("p (c t) -> p c t", c=8), op0=ALU.mult, op1=ALU.mult),
                    reads=[W["kkn"], W["e3"]], writes=[ar])
                P.op("gpsimd", lambda e_: e_.tensor_tensor(
                    out=ar[:, :, 1, :], in0=W["rm"][:, :].rearrange("p (c t) -> p c t", c=8),
                    in1=W["e1"][:, :].rearrange("p (c t) -> p c t", c=8), op=ALU.mult),
                    reads=[W["rm"], W["e1"]], writes=[ar])
                P.dma("sync", lambda e_, h=h, g=g: e_.dma_start(out=S["ar"][h, :, g * 8:(g + 1) * 8, :, :], in_=ar[:, :, :, :]),
                      reads=[ar], writes=[P.dbuf(("ar", g))])
                tt_("vector", W["bt"], W["bv"][:, :], W["e2"][:, :], ALU.mult, [W["bv"], W["e2"]])
                tt_("gpsimd", W["kt"], W["kp"][:, :], W["e2"][:, :], ALU.mult, [W["kp"], W["e2"]])
                P.dma("sync", lambda e_, h=h, c0g=c0g: e_.dma_start(out=S["bt"][h, :, c0g:c0g + 512], in_=W["bt"][:, :]),
                      reads=[W["bt"]], writes=[P.dbuf(("bt", g))])
                P.dma("sync", lambda e_, h=h, c0g=c0g: e_.dma_start(out=S["kt"][h, :, c0g:c0g + 512], in_=W["kt"][:, :]),
                      reads=[W["kt"]], writes=[P.dbuf(("kt", g))])
                P.op("scalar", lambda e_: e_.activation(
                    out=wcs[:, :], in_=W["e1"][:, :].rearrange("p (c t) -> p c t", c=8)[:, :, 63], func=AF.Copy),
                    reads=[W["e1"]], writes=[wcs])
                P.dma("sync", lambda e_, h=h, g=g: e_.dma_start(out=S["wc"][h, :, g * 8:(g + 1) * 8], in_=wcs[:, :]),
                      reads=[wcs], writes=[P.dbuf(("wc", g))])
                for q, (src, sap) in enumerate(((W["bt"], lambda ch: W["bt"][:, ch * 64:(ch + 1) * 64]),
                                                (W["kt"], lambda ch: W["kt"][:, ch * 64:(ch + 1) * 64]),
                                                (vall, lambda ch, h=h: vall[:, h, ch * 64:(ch + 1) * 64]))):
                    ps = pp.next()
                    for ch in range(8):
                        P.op("tensor", lambda e_, ps=ps, ch=ch, sap=sap: e_.transpose(
                            ps[:64, ch * 64:(ch + 1) * 64], sap(ch), ident[:64, :64]),
                            reads=[src, ident], writes=[ps])
                    P.op("scalar" if q != 1 else "vector",
                         (lambda e_, ps=ps, q=q: e_.activation(
                             out=tk[q][:, :, :], in_=ps[:64, :].rearrange("p (c j) -> p c j", c=8), func=AF.Copy))
                         if q != 1 else
                         (lambda e_, ps=ps, q=q: e_.tensor_copy(
                             out=tk[q][:, :, :], in_=ps[:64, :].rearrange("p (c j) -> p c j", c=8))),
                         reads=[ps], writes=[tk[q]])
                    P.dma("sync", lambda e_, h=h, g=g, q=q: e_.dma_start(
                        out=S["tok%d" % q][h, :, g * 8:(g + 1) * 8, :], in_=tk[q][:, :, :]),
                        reads=[tk[q]], writes=[P.dbuf(("tok%d" % q, g))])
            for cc in range(8):
                cp, ci2 = divmod(cc, 2)
                if ci2 == 0:
                    wb = wcnt % 2
                    wcnt += 1
                    for q, cbase in ((0, C_GB), (1, C_GC), (2, C_H)):
                        for k4 in range(4):
                            P.dma(WQ[0], lambda e_, k4=k4, wb=wb, cp=cp, q=q, cbase=cbase: e_.dma_start(
                                out=wq[q][wb][:, k4 * 4:(k4 + 1) * 4, :],
                                in_=w_in[:, k4 * 4:(k4 + 1) * 4, cbase + cp * 256:cbase + (cp + 1) * 256]),
                                writes=[wq[q][wb]])
                pgb, pgc, ph = pp.next(), pp.next(), pp.next()
                proj(pgb, wq[0][wb], ci2 * 128, 128)
                proj(pgc, wq[1][wb], ci2 * 128, 128)
                proj(ph, wq[2][wb], ci2 * 128, 128)
                P.op("scalar", lambda e_, ph=ph: e_.activation(out=hs_[:, :], in_=ph[:, :], func=AF.Copy),
                     reads=[ph], writes=[hs_])
                P.op("vector", lambda e_, pgc=pgc: e_.tensor_tensor(out=uc[:, 2:514], in0=pgc[:, :], in1=hs_[:, :],
                                                                   op=ALU.mult), reads=[pgc, hs_], writes=[uc])
                P.op("gpsimd", lambda e_, cc=cc: e_.tensor_copy(out=uc[:, 0:2], in_=ccarry[:, cc, :]),
                     reads=[ccarry], writes=[uc])
                P.op("gpsimd", lambda e_, cc=cc: e_.tensor_copy(out=ccarry[:, cc, :], in_=uc[:, 512:514]),
                     reads=[uc], writes=[ccarry])
                P.op("vector", lambda e_, cc=cc: e_.tensor_scalar(
                    out=acc[:, :], in0=uc[:, 2:514], scalar1=cw[:, 2, cc:cc + 1], scalar2=None, op0=ALU.mult),
                    reads=[uc, cw], writes=[acc])
                for k in (1, 0):
                    P.op("vector", lambda e_, cc=cc, k=k: e_.scalar_tensor_tensor(
                        out=acc[:, :], in0=uc[:, k:k + 512], scalar=cw[:, k, cc:cc + 1], in1=acc[:, :],
                        op0=ALU.mult, op1=ALU.add), reads=[uc, cw, acc], writes=[acc])
                P.op("vector", lambda e_, pgb=pgb: e_.tensor_tensor(out=ycb[:, :], in0=pgb[:, :], in1=acc[:, :],
                                                                   op=ALU.mult), reads=[pgb, acc], writes=[ycb])
                P.dma("sync", lambda e_, cc=cc, c0g=c0g: e_.dma_start(
                    out=S["catT"][RW + cc * 128:RW + (cc + 1) * 128, c0g:c0g + 512], in_=ycb[:, :]),
                    reads=[ycb], writes=[P.dbuf(("catT_c", g))])
    P.barrier()


def rwkv_scan_phase(P, C, e, T):
    nc = P.nc
    _pfx = uq("t")
    NCH = T // 64
    S = C.scr
    arv = S["ar"].rearrange("h j c a t -> j h c (a t)")
    btv = S["bt"].rearrange("h j t -> j h t")
    ktv = S["kt"].rearrange("h j t -> j h t")
    tokv = [S["tok%d" % q].rearrange("h t c j -> t h c j") for q in range(3)]
    gTv = S["gT"].rearrange("h i t -> i h t")
    bonv = S["bonT"].rearrange("h i t -> i h t")
    wcv = S["wc"].rearrange("h j c -> j h c")
    catv = S["catT"][0:RW, :].rearrange("(h i) t -> i h t", i=64)
    with ExitStack() as es:
        def A(name, shape, dt=F32):
            return Tl(es.enter_context(nc.sbuf_tensor(_pfx + name, shape, dt)))
        NB = 2
        arc = [A("arc%d" % i, [64, 16, 128]) for i in range(NB)]
        btc = [A("btc%d" % i, [64, 16, 64]) for i in range(NB)]
        ktc = [A("ktc%d" % i, [64, 16, 64]) for i in range(NB)]
        tkc = [[A("tkc%d_%d" % (q, i), [64, 16, 64]) for i in range(NB)] for q in range(3)]
        gtc = [A("gtc%d" % i, [64, 16, 64]) for i in range(NB)]
        bnc = [A("bnc%d" % i, [64, 16, 64]) for i in range(NB)]
        wca = A("wca", [64, 16, NCH])
        PST = [A("Pst%d" % i, [64, 8, 64]) for i in range(2)]
        TMP = []
        for i in range(2):
            TMP.append((A("ptmp%d" % i, [64, 8, 64]),
                        [A("Nm%d_%d" % (i, j), [64, 8, 64], BF16) for j in range(2)],
                        [A("NTm%d_%d" % (i, j), [64, 8, 64], BF16) for j in range(2)],
                        A("ABRB%d" % i, [64, 8, 2, 64]), A("AKRK%d" % i, [64, 8, 2, 64]),
                        A("TT%d" % i, [64, 8, 64]), A("X2%d" % i, [64, 8, 64]), A("Xs%d" % i, [64, 8, 64]),
                        A("Us%d" % i, [64, 8, 64]), A("ysb%d" % i, [64, 512]), A("dlt%d" % i, [64, 512]),
                        A("sq%d" % i, [64, 512]), A("rs%d" % i, [64, 512]), A("yo%d" % i, [64, 8, 64], BF16),
                        A("TTb%d" % i, [64, 8, 64], BF16)))
        pp_ = A("pp", [64, NPP])
        ones64 = A("ones64", [64, 64])
        maskL = A("maskL", [64, 64])
        mask2 = A("mask2", [64, 2, 64])
        id64 = A("id64", [64, 64])
        pp = PSPool(nc, es, _pfx, 8)
        P.dma("sync", lambda e_: e_.dma_start(out=pp_[:, :], in_=C.ev_pp[e]), writes=[pp_])
        P.dma("sync", lambda e_: e_.dma_start(out=ones64[:, :], in_=C.c_ones64), writes=[ones64])
        P.dma("sync", lambda e_: e_.dma_start(out=maskL[:, :], in_=C.c_maskL), writes=[maskL])
        P.dma("sync", lambda e_: e_.dma_start(out=mask2[:, :, :], in_=C.c_mask2), writes=[mask2])
        P.dma("sync", lambda e_: e_.dma_start(out=id64[:, :], in_=C.c_id64), writes=[id64])
        P.dma("sync", lambda e_: e_.dma_start(out=wca[:, :, :], in_=wcv),
              reads=[P.dbuf(("wc", g)) for g in range(T // 512)], writes=[wca])
        for i in range(2):
            P.op("gpsimd", lambda e_, i=i: e_.memset(PST[i][:, :, :], 0.0), writes=[PST[i]])

        def bc(ap2, n=64):
            return ap2.unsqueeze(2).broadcast_to([64, 8, n])

        def chunk_half(c, hf, b, g, t0):
            AR, BT, KT = arc[b], btc[b], ktc[b]
            BK, KK, VK = tkc[0][b], tkc[1][b], tkc[2][b]
            h0 = hf * 8
            Pst = PST[hf]
            ptmp, Nm, NTm, ABRB, AKRK, TT, X2, Xs, Us, ysb, dlt, sq, rs, yo, TTb = TMP[hf]
            psN, psAB0, psAB1, psAK0, psAK1 = pp.next(), pp.next(), pp.next(), pp.next(), pp.next()
            psAB = (psAB0, psAB1)
            psAK = (psAK0, psAK1)
            for hh in range(8):
                h = h0 + hh
                cs = slice(hh * 64, (hh + 1) * 64)
                c2 = slice((hh % 4) * 128, (hh % 4 + 1) * 128)
                P.op("tensor", lambda e_, h=h, cs=cs: e_.matmul(psN[:64, cs], lhsT=AR[:, h, 0:64], rhs=BT[:, h, :],
                                                               start=True, stop=True), reads=[AR, BT], writes=[psN])
                P.op("tensor", lambda e_, h=h, c2=c2, pt=psAB[hh // 4]: e_.matmul(
                    pt[:64, c2], lhsT=BT[:, h, :], rhs=AR[:, h, :], start=True, stop=True),
                    reads=[AR, BT], writes=[psAB[hh // 4]])
                P.op("tensor", lambda e_, h=h, c2=c2, pt=psAK[hh // 4]: e_.matmul(
                    pt[:64, c2], lhsT=KT[:, h, :], rhs=AR[:, h, :], start=True, stop=True),
                    reads=[AR, KT], writes=[psAK[hh // 4]])
            P.op("vector", lambda e_: e_.tensor_tensor(
                out=Nm[0][:, :, :], in0=psN[:64, :].rearrange("p (h s) -> p h s", h=8),
                in1=maskL[:, :].unsqueeze(1).broadcast_to([64, 8, 64]), op=ALU.mult),
                reads=[psN, maskL], writes=[Nm[0]])
            for q in range(2):
                P.op("vector", lambda e_, q=q: e_.tensor_tensor(
                    out=ABRB[:, q * 4:(q + 1) * 4, :, :],
                    in0=psAB[q][:64, :].rearrange("p (h a t) -> p h a t", h=4, a=2),
                    in1=mask2[:, :, :].unsqueeze(1).broadcast_to([64, 4, 2, 64]), op=ALU.mult),
                    reads=[psAB[q], mask2], writes=[ABRB])
                P.op("vector", lambda e_, q=q: e_.tensor_tensor(
                    out=AKRK[:, q * 4:(q + 1) * 4, :, :],
                    in0=psAK[q][:64, :].rearrange("p (h a t) -> p h a t", h=4, a=2),
                    in1=mask2[:, :, :].unsqueeze(1).broadcast_to([64, 4, 2, 64]), op=ALU.mult),
                    reads=[psAK[q], mask2], writes=[AKRK])
            yield
            P.op("gpsimd", lambda e_: e_.tensor_copy(out=NTm[0][:, :, :], in_=ABRB[:, :, 0, :]),
                 reads=[ABRB], writes=[NTm[0]])
            P.op("vector", lambda e_: e_.tensor_tensor(
                out=TT[:, :, :], in0=ABRB[:, :, 0, :], in1=id64[:, :].unsqueeze(1).broadcast_to([64, 8, 64]),
                op=ALU.add), reads=[ABRB, id64], writes=[TT])
            P.op("gpsimd", lambda e_: e_.tensor_copy(out=TTb[:, :, :], in_=TT[:, :, :]), reads=[TT], writes=[TTb])
            cur = 0
            for k in range(1, 6):
                nxt = 1 - cur
                psA, psB, psC = pp.next(), pp.next(), pp.next()
                for hh in range(8):
                    cs = slice(hh * 64, (hh + 1) * 64)
                    if k < 5:
                        P.op("tensor", lambda e_, hh=hh, cs=cs, cur=cur, psA=psA: e_.matmul(
                            psA[:64, cs], lhsT=Nm[cur][:, hh, :], rhs=NTm[cur][:, hh, :], start=True, stop=True),
                            reads=[Nm[cur], NTm[cur]], writes=[psA])
                    P.op("tensor", lambda e_, hh=hh, cs=cs, cur=cur, psB=psB: e_.matmul(
                        psB[:64, cs], lhsT=NTm[cur][:, hh, :], rhs=Nm[cur][:, hh, :], start=True, stop=True),
                        reads=[Nm[cur], NTm[cur]], writes=[psB])
                P.op("scalar", lambda e_, nxt=nxt, psB=psB: e_.activation(
                    out=Nm[nxt][:, :, :], in_=psB[:64, :].rearrange("p (h s) -> p h s", h=8), func=AF.Copy),
                    reads=[psB], writes=[Nm[nxt]])
                if k < 5:
                    P.op("vector", lambda e_, nxt=nxt, psA=psA: e_.tensor_copy(
                        out=NTm[nxt][:, :, :], in_=psA[:64, :].rearrange("p (h s) -> p h s", h=8)),
                        reads=[psA], writes=[NTm[nxt]])
                yield
                for hh in range(8):
                    cs = slice(hh * 64, (hh + 1) * 64)
                    P.op("tensor", lambda e_, hh=hh, cs=cs, nxt=nxt, psC=psC: e_.matmul(
                        psC[:64, cs], lhsT=Nm[nxt][:, hh, :], rhs=TTb[:, hh, :], start=True, stop=True),
                        reads=[Nm[nxt], TTb], writes=[psC])
                P.op("vector", lambda e_, psC=psC: e_.tensor_tensor(
                    out=TT[:, :, :], in0=TT[:, :, :], in1=psC[:64, :].rearrange("p (h s) -> p h s", h=8),
                    op=ALU.add), reads=[TT, psC], writes=[TT])
                if k < 5:
                    P.op("gpsimd", lambda e_: e_.tensor_copy(out=TTb[:, :, :], in_=TT[:, :, :]), reads=[TT], writes=[TTb])
                yield
                cur = nxt
            psX = pp.next()
            for hh in range(8):
                h = h0 + hh
                cs = slice(hh * 64, (hh + 1) * 64)
                P.op("tensor", lambda e_, hh=hh, h=h, cs=cs: e_.matmul(
                    psX[:64, cs], lhsT=AKRK[:, hh, 0, :], rhs=VK[:, h, :], start=True, stop=True),
                    reads=[AKRK, VK], writes=[psX])
            P.op("scalar", lambda e_: e_.activation(
                out=X2[:, :, :], in_=psX[:64, :].rearrange("p (h s) -> p h s", h=8), func=AF.Copy),
                reads=[psX], writes=[X2])
            yield
            ps1 = pp.next()
            for hh in range(8):
                h = h0 + hh
                cs = slice(hh * 64, (hh + 1) * 64)
                P.op("tensor", lambda e_, h=h, hh=hh, cs=cs: e_.matmul(
                    ps1[:64, cs], lhsT=AR[:, h, 0:64], rhs=Pst[:, hh, :], start=True, stop=True),
                    reads=[AR, Pst], writes=[ps1])
            P.op("vector", lambda e_: e_.tensor_tensor(
                out=Xs[:, :, :], in0=ps1[:64, :].rearrange("p (h s) -> p h s", h=8), in1=X2[:, :, :], op=ALU.add),
                reads=[ps1, X2], writes=[Xs])
            yield
            ps2 = pp.next()
            for hh in range(8):
                cs = slice(hh * 64, (hh + 1) * 64)
                P.op("tensor", lambda e_, hh=hh, cs=cs: e_.matmul(
                    ps2[:64, cs], lhsT=TT[:, hh, :], rhs=Xs[:, hh, :], start=True, stop=True),
                    reads=[TT, Xs], writes=[ps2])
            P.op("scalar", lambda e_: e_.activation(
                out=Us[:, :, :], in_=ps2[:64, :].rearrange("p (h s) -> p h s", h=8), func=AF.Copy),
                reads=[ps2], writes=[Us])
            yield
            psY = pp.next()
            for hh in range(8):
                h = h0 + hh
                cs = slice(hh * 64, (hh + 1) * 64)
                P.op("tensor", lambda e_, h=h, hh=hh, cs=cs: e_.matmul(
                    psY[:64, cs], lhsT=Pst[:, hh, :], rhs=AR[:, h, 64:128], start=True, stop=False),
                    reads=[AR, Pst], writes=[psY])
                P.op("tensor", lambda e_, hh=hh, cs=cs: e_.matmul(
                    psY[:64, cs], lhsT=Us[:, hh, :], rhs=ABRB[:, hh, 1, :], start=False, stop=False),
                    reads=[Us, ABRB], writes=[psY])
                P.op("tensor", lambda e_, hh=hh, h=h, cs=cs: e_.matmul(
                    psY[:64, cs], lhsT=VK[:, h, :], rhs=AKRK[:, hh, 1, :], start=False, stop=True),
                    reads=[VK, AKRK], writes=[psY])
            psP = pp.next()
            for hh in range(8):
                h = h0 + hh
                cs = slice(hh * 64, (hh + 1) * 64)
                P.op("tensor", lambda e_, hh=hh, h=h, cs=cs: e_.matmul(
                    psP[:64, cs], lhsT=BK[:, h, :], rhs=Us[:, hh, :], start=True, stop=False),
                    reads=[BK, Us], writes=[psP])
                P.op("tensor", lambda e_, h=h, cs=cs: e_.matmul(
                    psP[:64, cs], lhsT=KK[:, h, :], rhs=VK[:, h, :], start=False, stop=True),
                    reads=[KK, VK], writes=[psP])
            P.op("vector", lambda e_, h0=h0: e_.tensor_tensor(
                out=ptmp[:, :, :], in0=Pst[:, :, :], in1=psP[:64, :].rearrange("p (h s) -> p h s", h=8),
                op=ALU.add), reads=[Pst, psP], writes=[ptmp])
            P.op("vector", lambda e_, h0=h0, c=c: e_.tensor_tensor(
                out=Pst[:, :, :], in0=ptmp[:, :, :], in1=bc(wca[:, h0:h0 + 8, c]), op=ALU.mult),
                reads=[ptmp, wca], writes=[Pst])
            P.op("scalar", lambda e_: e_.activation(out=ysb[:, :], in_=psY[:64, :], func=AF.Copy),
                 reads=[psY], writes=[ysb])
            yield
            pm = pp.next()
            P.op("tensor", lambda e_, pm=pm: e_.matmul(pm[:64, :], lhsT=ones64[:, :], rhs=ysb[:, :], start=True, stop=True),
                 reads=[ones64, ysb], writes=[pm])
            P.op("vector", lambda e_, pm=pm: e_.scalar_tensor_tensor(
                out=dlt[:, :], in0=pm[:64, :], scalar=-1.0 / 64, in1=ysb[:, :], op0=ALU.mult, op1=ALU.add),
                reads=[pm, ysb], writes=[dlt])
            yield
            P.op("gpsimd", lambda e_: e_.tensor_tensor(out=sq[:, :], in0=dlt[:, :], in1=dlt[:, :], op=ALU.mult),
                 reads=[dlt], writes=[sq])
            pv = pp.next()
            P.op("tensor", lambda e_, pv=pv: e_.matmul(pv[:64, :], lhsT=ones64[:, :], rhs=sq[:, :], start=True, stop=True),
                 reads=[ones64, sq], writes=[pv])
            P.op("vector", lambda e_, pv=pv: e_.tensor_scalar(
                out=rs[:, :], in0=pv[:64, :], scalar1=1.0 / 64, scalar2=GN_EPS, op0=ALU.mult, op1=ALU.add),
                reads=[pv], writes=[rs])
            yield
            P.op("scalar", lambda e_: e_.activation(out=rs[:, :], in_=rs[:, :], func=AF.Sqrt), reads=[rs], writes=[rs])
            P.op("vector", lambda e_: e_.reciprocal(out=rs[:, :], in_=rs[:, :]), reads=[rs], writes=[rs])
            P.op("vector", lambda e_: e_.tensor_tensor(out=dlt[:, :], in0=dlt[:, :], in1=rs[:, :], op=ALU.mult),
                 reads=[dlt, rs], writes=[dlt])
            d3 = lambda: dlt[:, :].rearrange("p (h s) -> p h s", h=8)
            P.op("vector", lambda e_, h0=h0: e_.tensor_tensor(
                out=d3(), in0=d3(), in1=bc(pp_[:, PP_LG + h0:PP_LG + h0 + 8]), op=ALU.mult),
                reads=[dlt, pp_], writes=[dlt])
            P.op("vector", lambda e_, h0=h0: e_.tensor_tensor(
                out=d3(), in0=d3(), in1=bc(pp_[:, PP_LB + h0:PP_LB + h0 + 8]), op=ALU.add),
                reads=[dlt, pp_], writes=[dlt])
            P.op("gpsimd", lambda e_, h0=h0, b=b: e_.tensor_tensor(
                out=d3(), in0=d3(), in1=bnc[b][:, h0:h0 + 8, :], op=ALU.add), reads=[dlt, bnc[b]], writes=[dlt])
            P.op("vector", lambda e_, h0=h0, b=b: e_.tensor_tensor(
                out=yo[:, :, :], in0=d3(), in1=gtc[b][:, h0:h0 + 8, :], op=ALU.mult),
                reads=[dlt, gtc[b]], writes=[yo])
            P.dma("sync", lambda e_, h0=h0, t0=t0: e_.dma_start(out=catv[:, h0:h0 + 8, t0:t0 + 64], in_=yo[:, :, :]),
                  reads=[yo], writes=[P.dbuf(("catT_r", g))])

        for c in range(NCH):
            b = c % NB
            g = c // 8
            t0 = c * 64
            P.dma("sync", lambda e_, b=b, c=c: e_.dma_start(out=arc[b][:, :, :], in_=arv[:, :, c, :]),
                  reads=[P.dbuf(("ar", g))], writes=[arc[b]])
            P.dma("sync", lambda e_, b=b, t0=t0: e_.dma_start(out=btc[b][:, :, :], in_=btv[:, :, t0:t0 + 64]),
                  reads=[P.dbuf(("bt", g))], writes=[btc[b]])
            P.dma("sync", lambda e_, b=b, t0=t0: e_.dma_start(out=ktc[b][:, :, :], in_=ktv[:, :, t0:t0 + 64]),
                  reads=[P.dbuf(("kt", g))], writes=[ktc[b]])
            for q in range(3):
                P.dma("sync", lambda e_, b=b, c=c, q=q: e_.dma_start(out=tkc[q][b][:, :, :], in_=tokv[q][:, :, c, :]),
                      reads=[P.dbuf(("tok%d" % q, g))], writes=[tkc[q][b]])
            P.dma("sync", lambda e_, b=b, t0=t0: e_.dma_start(out=gtc[b][:, :, :], in_=gTv[:, :, t0:t0 + 64]),
                  reads=[P.dbuf(("gT", g))], writes=[gtc[b]])
            P.dma("sync", lambda e_, b=b, t0=t0: e_.dma_start(out=bnc[b][:, :, :], in_=bonv[:, :, t0:t0 + 64]),
                  reads=[P.dbuf(("bonT", g))], writes=[bnc[b]])
            gens = [chunk_half(c, hf, b, g, t0) for hf in range(2)]
            while gens:
                for gen in list(gens):
                    try:
                        next(gen)
                    except StopIteration:
                        gens.remove(gen)
    P.barrier()


FOX_SCALE = 128 ** -0.5


def fox_phase(P, C, o, Xin, xin_key, T):
    nc = P.nc
    _pfx = uq("t")
    NT = T // 128
    S = C.scr
    w_in = C.od_w_in[o].rearrange("(kc p) n -> p kc n", p=128)
    qTd, kTd, vd = S["qT"], S["kT"], S["vtok"]
    with ExitStack() as es:
        def A(name, shape, dt=F32):
            return Tl(es.enter_context(nc.sbuf_tensor(_pfx + name, shape, dt)))
        xs = [A("xs%d" % i, [128, D]) for i in range(2)]
        xT = A("xT", [128, 16, 512], BF16)
        wq = [A("wq%d" % i, [128, 16, 512], BF16) for i in range(2)]
        wf = A("wf", [128, 16, 16], BF16)
        ob = [A("ob%d" % i, [128, 512], BF16) for i in range(2)]
        bfb = A("bfb", [128, 16])
        tri = A("tri", [128, 128])
        ones = A("ones", [128, 128])
        onesb = A("onesb", [128, 128], BF16)
        maskb = A("maskb", [128, 128], BF16)
        spt = A("spt", [128, 16])
        CK = A("CK", [128, NT, 16])
        BASE = A("BASE", [128, NT + 1, 16])
        pp = PSPool(nc, es, _pfx, 8)
        P.dma("sync", lambda e_: e_.dma_start(out=bfb[:, :], in_=C.od_bf[o]), writes=[bfb])
        P.dma("sync", lambda e_: e_.dma_start(out=tri[:, :], in_=C.c_tri), writes=[tri])
        P.dma("sync", lambda e_: e_.dma_start(out=ones[:, :], in_=C.c_ones128), writes=[ones])
        P.dma(CQ[0], lambda e_: e_.dma_start(out=onesb[:, :], in_=C.c_ones128b), writes=[onesb])
        P.dma(CQ[0], lambda e_: e_.dma_start(out=maskb[:, :], in_=C.c_maskKQ), writes=[maskb])
        for k4 in range(4):
            P.dma(WQ[0], lambda e_, k4=k4: e_.dma_start(
                out=wf[:, k4 * 4:(k4 + 1) * 4, :], in_=w_in[:, k4 * 4:(k4 + 1) * 4, 3 * D:3 * D + 16],
                allow_slow_non_contiguous=True), writes=[wf])
        P.op("gpsimd", lambda e_: e_.memset(BASE[:, 0, :], 0.0), writes=[BASE])
        wcnt = 0
        ocnt = 0
        for g in range(T // 512):
            c0g = g * 512
            load_xT(P, C, Xin, xin_key, g, xs, xT, [pp.next() for _ in range(4)])
            for qk, dst, key in ((0, qTd, "qT"), (1, kTd, "kT")):
                for h in range(16):
                    hq, hh = divmod(h, 4)
                    if hh == 0:
                        wb = wcnt % 2
                        wcnt += 1
                        for k4 in range(4):
                            P.dma(WQ[0], lambda e_, k4=k4, wb=wb, qk=qk, hq=hq: e_.dma_start(
                                out=wq[wb][:, k4 * 4:(k4 + 1) * 4, :],
                                in_=w_in[:, k4 * 4:(k4 + 1) * 4, qk * D + hq * 512:qk * D + (hq + 1) * 512]),
                                writes=[wq[wb]])
                    ps = pp.next()
                    for kc in range(16):
                        P.op("tensor", lambda e_, ps=ps, kc=kc, wb=wb, hh=hh: e_.matmul(
                            ps[:, :], lhsT=wq[wb][:, kc, hh * 128:(hh + 1) * 128], rhs=xT[:, kc, :],
                            start=(kc == 0), stop=(kc == 15)), reads=[wq[wb], xT], writes=[ps])
                    b = ocnt % 2
                    ocnt += 1
                    P.op("scalar" if h % 2 else "vector",
                         (lambda e_, ps=ps, b=b: e_.activation(out=ob[b][:, :], in_=ps[:, :], func=AF.Copy)) if h % 2
                         else (lambda e_, ps=ps, b=b: e_.tensor_copy(out=ob[b][:, :], in_=ps[:, :])),
                         reads=[ps], writes=[ob[b]])
                    P.dma("sync", lambda e_, b=b, h=h, dst=dst, c0g=c0g: e_.dma_start(out=dst[h, :, c0g:c0g + 512], in_=ob[b][:, :]),
                          reads=[ob[b]], writes=[P.dbuf((key, g))])
            for fg in range(4):
                wb = wcnt % 2
                wcnt += 1
                for k4 in range(4):
                    P.dma(WQ[0], lambda e_, k4=k4, wb=wb, fg=fg: e_.dma_start(
                        out=wq[wb][:, k4 * 4:(k4 + 1) * 4, :],
                        in_=w_in[:, k4 * 4:(k4 + 1) * 4, 2 * D + fg * 512:2 * D + (fg + 1) * 512]), writes=[wq[wb]])
                for tt in range(4):
                    ps = pp.next()
                    for kc in range(16):
                        P.op("tensor", lambda e_, ps=ps, kc=kc, wb=wb, tt=tt: e_.matmul(
                            ps[:, :], lhsT=xT[:, kc, tt * 128:(tt + 1) * 128], rhs=wq[wb][:, kc, :],
                            start=(kc == 0), stop=(kc == 15)), reads=[wq[wb], xT], writes=[ps])
                    b = ocnt % 2
                    ocnt += 1
                    P.op("scalar" if tt % 2 else "vector",
                         (lambda e_, ps=ps, b=b: e_.activation(out=ob[b][:, :], in_=ps[:, :], func=AF.Copy)) if tt % 2
                         else (lambda e_, ps=ps, b=b: e_.tensor_copy(out=ob[b][:, :], in_=ps[:, :])),
                         reads=[ps], writes=[ob[b]])
                    r0 = c0g + tt * 128
                    P.dma("sync", lambda e_, b=b, r0=r0, fg=fg: e_.dma_start(
                        out=vd[r0:r0 + 128, fg * 512:(fg + 1) * 512], in_=ob[b][:, :]),
                        reads=[ob[b]], writes=[P.dbuf(("vtok", g))])
            for tt in range(4):
                j = g * 4 + tt
                ps = pp.next()
                for kc in range(16):
                    P.op("tensor", lambda e_, ps=ps, kc=kc, tt=tt: e_.matmul(
                        ps[:, 0:16], lhsT=xT[:, kc, tt * 128:(tt + 1) * 128], rhs=wf[:, kc, :],
                        start=(kc == 0), stop=(kc == 15)), reads=[wf, xT], writes=[ps])
                P.op("vector", lambda e_, ps=ps: e_.tensor_tensor(out=spt[:, :], in0=ps[:, 0:16], in1=bfb[:, :], op=ALU.add),
                     reads=[ps, bfb], writes=[spt])
                P.op("scalar", lambda e_: e_.activation(out=spt[:, :], in_=spt[:, :], func=AF.Exp, scale=-1.0),
                     reads=[spt], writes=[spt])
                P.op("scalar", lambda e_: e_.activation(out=spt[:, :], in_=spt[:, :], func=AF.Ln, bias=1.0),
                     reads=[spt], writes=[spt])
                ps2 = pp.next()
                P.op("tensor", lambda e_, ps2=ps2: e_.matmul(ps2[:, 0:16], lhsT=tri[:, :], rhs=spt[:, :], start=True, stop=True),
                     reads=[tri, spt], writes=[ps2])
                P.op("tensor", lambda e_, ps2=ps2: e_.matmul(ps2[:, 16:32], lhsT=ones[:, :], rhs=spt[:, :], start=True, stop=True),
                     reads=[ones, spt], writes=[ps2])
                P.op("vector", lambda e_, ps2=ps2, j=j: e_.tensor_tensor(
                    out=CK[:, j, :], in0=ps2[:, 0:16], in1=BASE[:, j, :], op=ALU.add), reads=[ps2, BASE], writes=[CK])
                P.op("vector", lambda e_, ps2=ps2, j=j: e_.tensor_tensor(
                    out=BASE[:, j + 1, :], in0=ps2[:, 16:32], in1=BASE[:, j, :], op=ALU.add), reads=[ps2, BASE], writes=[BASE])
        qh = A("qh", [128, T], BF16)
        kh = A("kh", [128, T], BF16)
        vh = A("vh", [128, NT, 128], BF16)
        pT = [A("pT%d" % i, [128, 512], BF16) for i in range(2)]
        bia = [A("bia%d" % i, [128, 4]) for i in range(2)]
        rden = A("rden", [128, 512])
        oTb = A("oTb", [128, 512], BF16)
        allg = list(range(T // 512))
        vview = vd.rearrange("(j p) n -> p j n", p=128)
        pcnt = 0
        ocnt2 = 0
        for h in range(16):
            P.dma("sync", lambda e_, h=h: e_.dma_start(out=qh[:, :], in_=qTd[h, :, :]),
                  reads=[P.dbuf(("qT", g)) for g in allg], writes=[qh])
            P.dma("sync", lambda e_, h=h: e_.dma_start(out=kh[:, :], in_=kTd[h, :, :]),
                  reads=[P.dbuf(("kT", g)) for g in allg], writes=[kh])
            P.dma("sync", lambda e_, h=h: e_.dma_start(out=vh[:, :, :], in_=vview[:, :, h * 128:(h + 1) * 128]),
                  reads=[P.dbuf(("vtok", g)) for g in allg], writes=[vh])
            for G in range(T // 512):
                po, pden = pp.t[4 + 2 * (ocnt2 % 2)], pp.t[5 + 2 * (ocnt2 % 2)]
                ocnt2 += 1
                nkb = 4 * G + 4
                for kb in range(nkb):
                    ps = pp.t[pcnt % 4]
                    P.op("tensor", lambda e_, ps=ps, kb=kb, G=G: e_.matmul(
                        ps[:, :], lhsT=kh[:, kb * 128:(kb + 1) * 128], rhs=qh[:, G * 512:(G + 1) * 512],
                        start=True, stop=True), reads=[kh, qh], writes=[ps])
                    b = pcnt % 2
                    pcnt += 1
                    P.op("vector", lambda e_, b=b, kb=kb, G=G, h=h: e_.tensor_scalar(
                        out=bia[b][:, :], in0=BASE[:, 4 * G + 1:4 * G + 5, h], scalar1=-1.0,
                        scalar2=CK[:, kb, h:h + 1], op0=ALU.mult, op1=ALU.add), reads=[BASE, CK], writes=[bia[b]])
                    for j in range(4):
                        qb = 4 * G + j
                        cs = slice(j * 128, (j + 1) * 128)
                        if kb > qb:
                            P.op("gpsimd", lambda e_, b=b, cs=cs: e_.memset(pT[b][:, cs], 0.0), writes=[pT[b]])
                            continue
                        P.op("scalar", lambda e_, ps=ps, b=b, cs=cs, j=j: e_.activation(
                            out=pT[b][:, cs], in_=ps[:, cs], func=AF.Exp, bias=bia[b][:, j:j + 1], scale=FOX_SCALE),
                            reads=[ps, bia[b]], writes=[pT[b]], skip_self=True)
                        if kb == qb:
                            P.op("gpsimd", lambda e_, b=b, cs=cs: e_.tensor_tensor(
                                out=pT[b][:, cs], in0=pT[b][:, cs], in1=maskb[:, :], op=ALU.mult),
                                reads=[pT[b], maskb], writes=[pT[b]])
                    P.op("tensor", lambda e_, b=b, kb=kb, nkb=nkb, po=po: e_.matmul(
                        po[:, :], lhsT=vh[:, kb, :], rhs=pT[b][:, :], start=(kb == 0), stop=(kb == nkb - 1)),
                        reads=[vh, pT[b]], writes=[po])
                    P.op("tensor", lambda e_, b=b, kb=kb, nkb=nkb, pden=pden: e_.matmul(
                        pden[:, :], lhsT=onesb[:, :], rhs=pT[b][:, :], start=(kb == 0), stop=(kb == nkb - 1)),
                        reads=[onesb, pT[b]], writes=[pden])
                P.op("vector", lambda e_, pden=pden: e_.reciprocal(out=rden[:, :], in_=pden[:, :]), reads=[pden], writes=[rden])
                P.op("vector", lambda e_, po=po: e_.tensor_tensor(out=oTb[:, :], in0=po[:, :], in1=rden[:, :], op=ALU.mult),
                     reads=[po, rden], writes=[oTb])
                P.dma("sync", lambda e_, h=h, G=G: e_.dma_start(
                    out=S["catT"][h * 128:(h + 1) * 128, G * 512:(G + 1) * 512], in_=oTb[:, :]),
                    reads=[oTb], writes=[P.dbuf(("catT_f", G))])
    P.barrier()


def build(T, layers, mode="full", dbg_out=()):
    nc = bass.Bass("TRN2", target_bir_lowering=False)
    C = Ctx()
    NCH = T // 64

    def inp(name, shape, dt=F32):
        return nc.dram_tensor(name, list(shape), dt, kind="ExternalInput").ap()

    def scr(name, shape, dt=F32):
        return nc.dram_tensor(name, list(shape), dt, kind="Internal").ap()

    C.x = inp("x", [T, D])
    identd = inp("ident", [128, 128])
    C.ln_g = inp("ln_g", [DEPTH, 2, 128, D])
    C.ln_b = inp("ln_b", [DEPTH, 2, 128, D])
    C.ff_w_up = {l: inp("ff_w_up%d" % l, [D, 2 * DFF], WDT[0]) for l in layers}
    C.ff_w_down = {l: inp("ff_w_down%d" % l, [DFF, D], WDT[0]) for l in layers}
    C.ff_cwb = inp("ff_cwb", [DEPTH, 128, 4, 2 * NFC])
    evs = sorted({l // 2 for l in layers if l % 2 == 0})
    ods = sorted({l // 2 for l in layers if l % 2 == 1})
    if mode != "ffn_only":
        C.c_ones64 = inp("c_ones64", [64, 64])
        C.c_rst = inp("c_rst", [64, 512])
        C.c_maskL = inp("c_maskL", [64, 64])
        C.c_mask2 = inp("c_mask2", [64, 2, 64])
        C.c_id64 = inp("c_id64", [64, 64])
        C.c_tri = inp("c_tri", [128, 128])
        C.c_ones128 = inp("c_ones128", [128, 128])
        C.c_ones128b = inp("c_ones128b", [128, 128], WDT[0])
        C.c_maskKQ = inp("c_maskKQ", [128, 128], WDT[0])
        C.ev_w_in = {e: inp("ev_w_in%d" % e, [D, 6432], WDT[0]) for e in evs}
        C.ev_w_out = {e: inp("ev_w_out%d" % e, [D, D], WDT[0]) for e in evs}
        C.ev_pp = {e: inp("ev_pp%d" % e, [64, NPP]) for e in evs}
        C.ev_w2 = {e: inp("ev_w2_%d" % e, [64, 1024], WDT[0]) for e in evs}
        C.ev_a2 = {e: inp("ev_a2_%d" % e, [64, 1024], WDT[0]) for e in evs}
        C.ev_g2 = {e: inp("ev_g2_%d" % e, [64, 3, 1024], WDT[0]) for e in evs}
        C.ev_rkmat = {e: inp("ev_rkmat%d" % e, [64, 16, 64]) for e in evs}
        C.ev_cw = {e: inp("ev_cw%d" % e, [128, 3, 8]) for e in evs}
        C.ev_v1 = {e - 1: inp("ev_v1_%d" % (e - 1), [64, 16, 32]) for e in evs if e > 0}
        C.ev_v2 = {e - 1: inp("ev_v2_%d" % (e - 1), [32, 1024]) for e in evs if e > 0}
        C.od_w_in = {o: inp("od_w_in%d" % o, [D, 3 * D + 16], WDT[0]) for o in ods}
        C.od_w_out = {o: inp("od_w_out%d" % o, [D, D], WDT[0]) for o in ods}
        C.od_bf = {o: inp("od_bf%d" % o, [128, 16]) for o in ods}
        S = {}
        S["catT"] = scr("catT", [D, T], BF16)
        if evs:
            for n in ("vfirst", "gT", "bonT", "bt", "kt"):
                S[n] = scr(n, [16, 64, T])
            S["ar"] = scr("ar", [16, 64, NCH, 2, 64])
            for q in range(3):
                S["tok%d" % q] = scr("tok%d" % q, [16, 64, NCH, 64])
            S["wc"] = scr("wc", [16, 64, NCH])
        if ods:
            S["qT"] = scr("qT", [16, 128, T], BF16)
            S["kT"] = scr("kT", [16, 128, T], BF16)
            S["vtok"] = scr("vtok", [T, D], BF16)
        C.scr = S
    out = nc.dram_tensor("out", [T, D], F32, kind="ExternalOutput").ap()
    xa = scr("xa", [T, D])
    xb = scr("xb", [T, D])
    P = Prog(nc)
    C.dbg = None
    if mode == "ffn_only":
        C.dbg = {"xT": nc.dram_tensor("d_xT", [128, 16, 512], BF16, kind="ExternalOutput").ap(),
                 "hT": nc.dram_tensor("d_hT", [128, NFC, 512], BF16, kind="ExternalOutput").ap(),
                 "z": nc.dram_tensor("d_z", [128, D], F32, kind="ExternalOutput").ap()}
    with ExitStack() as es:
        C.ident = Tl(es.enter_context(nc.sbuf_tensor("ident_sb", [128, 128], F32)))
        P.dma("sync", lambda e: e.dma_start(out=C.ident[:, :], in_=identd), writes=[C.ident])
        if WDT[0] == F32:
            pairs = []
            dicts = [C.ff_w_up, C.ff_w_down]
            if mode != "ffn_only":
                dicts += [C.ev_w_in, C.ev_w_out, C.od_w_in, C.od_w_out]
            for di, dct in enumerate(dicts):
                for key in sorted(dct):
                    src = dct[key]
                    dst = scr("wbf_%d_%d" % (di, key), list(src.shape), BF16)
                    pairs.append((src, dst))
                    dct[key] = dst
            precast_phase(P, C, pairs)
        if mode == "ffn_only":
            ffn_phase(P, C, layers[0], C.x, "x", out, "out", T)
        else:
            cur, ckey = C.x, "x"
            for li, l in enumerate(layers):
                last = li == len(layers) - 1
                if l % 2 == 0:
                    e = l // 2
                    rwkv_prep_phase(P, C, e, l, cur, ckey, T)
                    if mode == "prep_only":
                        break
                    rwkv_scan_phase(P, C, e, T)
                    if mode == "scan_only":
                        break
                    srckeys, w_out = "cat_e%d" % l, C.ev_w_out[e]
                else:
                    o = l // 2
                    fox_phase(P, C, o, cur, ckey, T)
                    w_out = C.od_w_out[o]
                if mode == "mixer_only":
                    outproj_phase(P, C, C.scr["catT"], None, w_out, l, cur, ckey, out, "out", T)
                    break
                outproj_phase(P, C, C.scr["catT"], None, w_out, l, cur, ckey, xa, "xa%d" % l, T)
                dst, dkey = (out, "out") if last else (xb, "xb%d" % l)
                ffn_phase(P, C, l, xa, "xa%d" % l, dst, dkey, T)
                cur, ckey = dst, dkey
        for n in dbg_out:
            src = C.scr[n]
            dst = nc.dram_tensor("dbg_" + n, list(src.shape), src.tensor.dtype, kind="ExternalOutput").ap()
            P.dma("sync", lambda e, src=src, dst=dst: e.dma_start(out=dst, in_=src))
        P.emit()
    return nc


def host_inputs(inputs, T, layers=(0, 1, 2, 3), mode="full"):
    f = np.float32
    m = {}
    m["ident"] = np.eye(128, dtype=f)
    m["ln_g"] = np.ascontiguousarray(np.broadcast_to(inputs["ln_g"][:, :, None, :], (DEPTH, 2, 128, D))).astype(f)
    m["ln_b"] = np.ascontiguousarray(np.broadcast_to(inputs["ln_b"][:, :, None, :], (DEPTH, 2, 128, D))).astype(f)
    for l in layers:
        m["ff_w_up%d" % l] = np.ascontiguousarray(inputs["ff_w_up"][l], dtype=f)
        m["ff_w_down%d" % l] = np.ascontiguousarray(inputs["ff_w_down"][l], dtype=f)
    cw = np.concatenate([inputs["ff_conv_w"], inputs["ff_conv_b"][:, None, :]], axis=1)
    m["ff_cwb"] = np.ascontiguousarray(cw.reshape(DEPTH, 4, 2 * NFC, 128).transpose(0, 3, 1, 2)).astype(f)
    if mode == "ffn_only":
        return m
    p = np.arange(64)
    m["c_ones64"] = np.ones((64, 64), f)
    rst = np.ones((64, 512), f)
    rst[:, ::64] = 0.0
    m["c_rst"] = rst
    m["c_maskL"] = (p[:, None] > p[None, :]).astype(f)
    m["c_mask2"] = np.ascontiguousarray(np.stack([(p[:, None] < p[None, :]), (p[:, None] <= p[None, :])], 1)).astype(f)
    m["c_id64"] = np.eye(64, dtype=f)
    q = np.arange(128)
    m["c_tri"] = (q[:, None] <= q[None, :]).astype(f)
    m["c_ones128"] = np.ones((128, 128), f)
    m["c_ones128b"] = np.ones((128, 128), f)
    m["c_maskKQ"] = (q[:, None] <= q[None, :]).astype(f)
    hd = lambda v: np.ascontiguousarray(np.asarray(v, f).reshape(16, 64).T)
    for e in sorted({l // 2 for l in layers if l % 2 == 0}):
        m["ev_w_in%d" % e] = np.ascontiguousarray(inputs["ev_w_in"][e], dtype=f)
        m["ev_w_out%d" % e] = np.ascontiguousarray(inputs["ev_w_out"][e], dtype=f)
        mu = np.asarray(inputs["ev_mu"][e], f)
        pp = np.zeros((64, NPP), f)
        pp[:, PP_MUR:PP_MUR + 16] = hd(mu[0:1024])
        pp[:, PP_MUK:PP_MUK + 16] = hd(mu[1024:2048])
        pp[:, PP_MUV:PP_MUV + 16] = hd(mu[2048:3072])
        pp[:, PP_W0:PP_W0 + 16] = hd(inputs["ev_w0"][e])
        pp[:, PP_A0:PP_A0 + 16] = hd(inputs["ev_a0"][e])
        pp[:, PP_KK:PP_KK + 16] = hd(inputs["ev_k_k"][e])
        pp[:, PP_KA:PP_KA + 16] = hd(inputs["ev_k_a"][e])
        pp[:, PP_LG:PP_LG + 16] = hd(inputs["ev_lnx_g"][e])
        pp[:, PP_LB:PP_LB + 16] = hd(inputs["ev_lnx_b"][e])
        if e > 0:
            pp[:, PP_V0:PP_V0 + 16] = hd(inputs["ev_v0"][e - 1])
        pp[:, PP_MUDW] = mu[3072:3136]
        pp[:, PP_MUDA] = mu[3136:3200]
        pp[:, PP_MUDG] = mu[3200:3264]
        pp[:, PP_MUDG + 1] = mu[3264:3328]
        pp[:32, PP_MUDG + 2] = mu[3328:3360]
        m["ev_pp%d" % e] = pp
        m["ev_w2_%d" % e] = np.ascontiguousarray(inputs["ev_w2"][e], dtype=f)
        m["ev_a2_%d" % e] = np.ascontiguousarray(inputs["ev_a2"][e], dtype=f)
        g2 = np.zeros((192, 1024), f)
        g2[:160] = inputs["ev_g2"][e]
        m["ev_g2_%d" % e] = np.ascontiguousarray(g2.reshape(3, 64, 1024).transpose(1, 0, 2))
        rk = np.asarray(inputs["ev_r_k"][e], f)
        m["ev_rkmat%d" % e] = np.ascontiguousarray(np.broadcast_to(rk.T[:, :, None], (64, 16, 64))).astype(f)
        cwv = np.asarray(inputs["ev_conv_w"][e], f)
        m["ev_cw%d" % e] = np.ascontiguousarray(cwv.reshape(3, 8, 128).transpose(2, 0, 1))
        if e > 0:
            v1 = np.asarray(inputs["ev_v1"][e - 1], f)
            m["ev_v1_%d" % (e - 1)] = np.ascontiguousarray(v1.reshape(16, 64, 32).transpose(1, 0, 2))
            m["ev_v2_%d" % (e - 1)] = np.ascontiguousarray(inputs["ev_v2"][e - 1], dtype=f)
    for o in sorted({l // 2 for l in layers if l % 2 == 1}):
        m["od_w_in%d" % o] = np.ascontiguousarray(inputs["od_w_in"][o], dtype=f)
        m["od_w_out%d" % o] = np.ascontiguousarray(inputs["od_w_out"][o], dtype=f)
        m["od_bf%d" % o] = np.ascontiguousarray(np.broadcast_to(np.asarray(inputs["od_b_f"][o], f)[None, :], (128, 16)))
    return m


SEQ = 4096
N_ACTIVE = 4


def kernel(**inputs):
    x = np.asarray(inputs["x"], np.float32)
    B = x.shape[0]
    nc = build(SEQ, [0, 1, 2, 3])
    base = host_inputs(inputs, SEQ)
    in_maps = []
    for c in range(N_ACTIVE):
        m = dict(base)
        m["x"] = np.ascontiguousarray(x[c % B])
        in_maps.append(m)
    res = run_bass_kernel_spmd(nc, in_maps, core_ids=list(range(N_ACTIVE)))
    out = np.stack([np.asarray(res.results[b]["out"], np.float32) for b in range(B)], 0)
    return out
```

```python
from contextlib import ExitStack
import numpy as np
import concourse.bass as bass
import concourse.mybir as mybir
from concourse.bass_utils import run_bass_kernel_spmd

F32 = mybir.dt.float32
BF16 = mybir.dt.bfloat16
AF = mybir.ActivationFunctionType
ALU = mybir.AluOpType

D = 2048
DFF = 5632
NFC = DFF // 128
DEPTH = 4
ALPHA = float((2 * DEPTH) ** 0.25)
LN_EPS = 1e-5
NDMA = 40
ENGS = ("tensor", "vector", "scalar", "gpsimd", "sync")
WQ = ["sync"]
CQ = ["gpsimd"]
WDT = [F32]


class Buf:
    __slots__ = ("w", "r")

    def __init__(self):
        self.w = None
        self.r = {}


class Tl:
    def __init__(self, t):
        self.t = t
        self.b = Buf()

    def __getitem__(self, k):
        return self.t[k]


class Prog:
    def __init__(self, nc):
        self.nc = nc
        self.ops = {e: [] for e in ENGS}
        self.cnt = {}
        self.seen = {e: {} for e in ENGS}
        self.rr = 0
        self.nonce = 0
        self.dram = {}

    def dbuf(self, key):
        if key not in self.dram:
            self.dram[key] = Buf()
        return self.dram[key]

    def _deps(self, eng, reads, writes, skip_self=False):
        need = {}

        def add(dep):
            if dep is None:
                return
            k, c = dep
            if k == eng and (eng == "tensor" or skip_self):
                return
            if need.get(k, 0) < c:
                need[k] = c

        for b in reads:
            add(b.w)
        for b in writes:
            add(b.w)
            for k, c in b.r.items():
                add((k, c))
        out = []
        for k, c in need.items():
            if self.seen[eng].get(k, 0) < c:
                self.seen[eng][k] = c
                out.append((k, c))
        return out

    @staticmethod
    def _bufs(xs):
        return [x.b if isinstance(x, Tl) else x for x in xs]

    def op(self, eng, fn, reads=(), writes=(), skip_self=False):
        reads = self._bufs(reads)
        writes = self._bufs(writes)
        waits = self._deps(eng, reads, writes, skip_self)
        c = self.cnt.get(eng, 0) + 1
        self.cnt[eng] = c
        self.ops[eng].append((waits, fn, (eng, 1)))
        for b in reads:
            b.r[eng] = c
        for b in writes:
            b.w = (eng, c)
            b.r = {}

    def dma(self, q, fn, reads=(), writes=()):
        reads = self._bufs(reads)
        writes = self._bufs(writes)
        if q == "gpsimd":
            k = "once%d" % self.nonce
            self.nonce += 1
        else:
            k = "dma%d" % self.rr
            self.rr = (self.rr + 1) % NDMA
        waits = self._deps(q, reads, writes)
        prev = self.cnt.get(k, 0)
        if prev and self.seen[q].get(k, 0) < prev:
            self.seen[q][k] = prev
            waits.append((k, prev))
        c = prev + 16
        self.cnt[k] = c
        self.ops[q].append((waits, fn, (k, 16)))
        for b in reads:
            b.r[k] = c
        for b in writes:
            b.w = (k, c)
            b.r = {}

    def barrier(self):
        snap = dict(self.cnt)
        for e in ENGS:
            waits = []
            for k, c in snap.items():
                if k != e and self.seen[e].get(k, 0) < c:
                    self.seen[e][k] = c
                    waits.append((k, c))
            if waits:
                self.ops[e].append((waits, None, None))

    def emit(self):
        nc = self.nc
        keys = list(ENGS) + ["dma%d" % i for i in range(NDMA)] + ["once%d" % i for i in range(self.nonce)]
        with ExitStack() as es:
            sems = {k: es.enter_context(nc.semaphore("s_" + k)) for k in keys}
            block = es.enter_context(nc.Block())

            def run(eng, e):
                for waits, fn, inc in self.ops[eng]:
                    for k, c in waits:
                        e.wait_ge(sems[k], c)
                    if fn is not None:
                        ins = fn(e)
                        ins.then_inc(sems[inc[0]], inc[1])
                if eng == "sync":
                    for k, c in self.cnt.items():
                        if (k.startswith("dma") or k.startswith("once")) and self.seen[eng].get(k, 0) < c:
                            e.wait_ge(sems[k], c)

            @block.tensor
            def _(e):
                run("tensor", e)

            @block.vector
            def _(e):
                run("vector", e)

            @block.scalar
            def _(e):
                run("scalar", e)

            @block.gpsimd
            def _(e):
                run("gpsimd", e)

            @block.sync
            def _(e):
                run("sync", e)


class Ctx:
    pass


_UID = [0]


def uq(prefix):
    _UID[0] += 1
    return "%s%d_" % (prefix, _UID[0])


def load_xT(P, C, Xin, xkey, g, xs, xT, pst):
    ident = C.ident
    for tt in range(4):
        r0 = g * 512 + tt * 128
        pass
    i = 0
    nx = len(xs)
    for tt in range(4):
        r0 = g * 512 + tt * 128
        P.dma("sync", lambda e, tt=tt, r0=r0: e.dma_start(out=xs[tt % nx][:, :], in_=Xin[r0:r0 + 128, :]),
              reads=[P.dbuf((xkey, g))], writes=[xs[tt % nx]])
        for k4 in range(4):
            ps = pst[i % len(pst)]
            i += 1
            for j in range(4):
                kc = k4 * 4 + j
                P.op("tensor", lambda e, ps=ps, tt=tt, kc=kc, j=j: e.transpose(
                    ps[:, j * 128:(j + 1) * 128], xs[tt % nx][:, kc * 128:(kc + 1) * 128], ident[:, :]),
                    reads=[xs[tt % nx], ident], writes=[ps])
            eng = "scalar" if (i % 2) else "vector"
            if eng == "scalar":
                P.op("scalar", lambda e, ps=ps, tt=tt, k4=k4: e.activation(
                    out=xT[:, k4 * 4:(k4 + 1) * 4, tt * 128:(tt + 1) * 128],
                    in_=ps[:, :].rearrange("p (a b) -> p a b", a=4), func=AF.Copy),
                    reads=[ps], writes=[xT])
            else:
                P.op("vector", lambda e, ps=ps, tt=tt, k4=k4: e.tensor_copy(
                    out=xT[:, k4 * 4:(k4 + 1) * 4, tt * 128:(tt + 1) * 128],
                    in_=ps[:, :].rearrange("p (a b) -> p a b", a=4)),
                    reads=[ps], writes=[xT])


def resid_ln(P, C, xs_tt, lnG, lnB, sm):
    junk, mv, rs, nmr = sm
    AXX = mybir.AxisListType.X
    P.op("vector", lambda e: e.reduce_sum(out=mv[:, 0:1], in_=xs_tt[:, :], axis=AXX), reads=[xs_tt], writes=[mv])
    P.op("gpsimd", lambda e: e.tensor_tensor(out=junk[:, :], in0=xs_tt[:, :], in1=xs_tt[:, :], op=ALU.mult),
         reads=[xs_tt], writes=[junk])
    P.op("vector", lambda e: e.reduce_sum(out=mv[:, 1:2], in_=junk[:, :], axis=AXX), reads=[junk], writes=[mv])
    P.op("vector", lambda e: e.tensor_scalar(out=mv[:, 0:1], in0=mv[:, 0:1], scalar1=1.0 / D, scalar2=None,
                                             op0=ALU.mult), reads=[mv], writes=[mv])
    P.op("vector", lambda e: e.tensor_tensor(out=rs[:, :], in0=mv[:, 0:1], in1=mv[:, 0:1], op=ALU.mult),
         reads=[mv], writes=[rs])
    P.op("vector", lambda e: e.scalar_tensor_tensor(out=rs[:, :], in0=mv[:, 1:2], scalar=1.0 / D, in1=rs[:, :],
                                                    op0=ALU.mult, op1=ALU.subtract), reads=[mv, rs], writes=[rs])
    P.op("vector", lambda e: e.tensor_scalar(out=rs[:, :], in0=rs[:, :], scalar1=LN_EPS, scalar2=None,
                                             op0=ALU.add), reads=[rs], writes=[rs])
    P.op("scalar", lambda e: e.activation(out=rs[:, :], in_=rs[:, :], func=AF.Sqrt), reads=[rs], writes=[rs])
    P.op("vector", lambda e: e.reciprocal(out=rs[:, :], in_=rs[:, :]), reads=[rs], writes=[rs])
    P.op("vector", lambda e: e.tensor_scalar(out=nmr[:, :], in0=mv[:, 0:1], scalar1=rs[:, 0:1], scalar2=-1.0,
                                             op0=ALU.mult, op1=ALU.mult), reads=[mv, rs], writes=[nmr])
    P.op("scalar", lambda e: e.activation(out=xs_tt[:, :], in_=xs_tt[:, :], func=AF.Identity,
                                          bias=nmr[:, 0:1], scale=rs[:, 0:1]),
         reads=[xs_tt, rs, nmr], writes=[xs_tt])
    P.op("vector", lambda e: e.tensor_tensor(out=xs_tt[:, :], in0=xs_tt[:, :], in1=lnG[:, :], op=ALU.mult),
         reads=[xs_tt, lnG], writes=[xs_tt])
    P.op("gpsimd", lambda e: e.tensor_tensor(out=xs_tt[:, :], in0=xs_tt[:, :], in1=lnB[:, :], op=ALU.add),
         reads=[xs_tt, lnB], writes=[xs_tt])


def ffn_phase(P, C, l, Xin, xin_key, Xout, xout_key, T):
    nc = P.nc
    _pfx = uq("t")
    w_up = C.ff_w_up[l].rearrange("(kc p) n -> p kc n", p=128)
    w_dn = C.ff_w_down[l].rearrange("(c p) n -> p c n", p=128)
    with ExitStack() as es:
        def A(name, shape, dt):
            return Tl(es.enter_context(nc.sbuf_tensor(_pfx + name, shape, dt)))

        def PS(name):
            return Tl(es.enter_context(nc.psum_tensor(_pfx + "p" + name, [128, 512], F32)))

        xs = [A("xs%d" % i, [128, D], F32) for i in range(4)]
        xT = A("xT", [128, 16, 512], BF16)
        wg = [A("wg%d" % i, [128, 16, 256], BF16) for i in range(2)]
        wv = [A("wv%d" % i, [128, 16, 256], BF16) for i in range(2)]
        hT = A("hT", [128, NFC, 512], BF16)
        wd = [A("wd%d" % i, [128, 4, 512], BF16) for i in range(3)]
        ug = [A("ug%d" % i, [128, 514], F32) for i in range(2)]
        uv = [A("uv%d" % i, [128, 514], F32) for i in range(2)]
        ag = [A("ag%d" % i, [128, 512], F32) for i in range(2)]
        av = [A("av%d" % i, [128, 512], F32) for i in range(2)]
        carry = A("carry", [128, 2 * NFC, 2], F32)
        cwb = A("cwb", [128, 4, 2 * NFC], F32)
        lnG = A("lnG", [128, D], F32)
        lnB = A("lnB", [128, D], F32)
        sm = (A("junk", [128, D], F32), A("mv", [128, 2], F32), A("rs", [128, 1], F32), A("nmr", [128, 1], F32))
        pg = [PS("g%d" % i) for i in range(2)]
        pv = [PS("v%d" % i) for i in range(2)]
        pd = [PS("d%d" % i) for i in range(4)]

        P.dma("sync", lambda e: e.dma_start(out=cwb[:, :, :], in_=C.ff_cwb[l]), writes=[cwb])
        P.dma("sync", lambda e: e.dma_start(out=lnG[:, :], in_=C.ln_g[l, 1]), writes=[lnG])
        P.dma("sync", lambda e: e.dma_start(out=lnB[:, :], in_=C.ln_b[l, 1]), writes=[lnB])
        P.op("gpsimd", lambda e: e.memset(carry[:, :, :], 0.0), writes=[carry])

        wcount = 0
        dcount = 0
        for g in range(T // 512):
            load_xT(P, C, Xin, xin_key, g, xs, xT, pg + pv)
            for c in range(NFC):
                cq, j = divmod(c, 2)
                if j == 0:
                    wb = wcount % 2
                    wcount += 1
                    for k4 in range(2):
                        P.dma(WQ[0], lambda e, wb=wb, k4=k4, cq=cq: e.dma_start(
                            out=wg[wb][:, k4 * 8:(k4 + 1) * 8, :],
                            in_=w_up[:, k4 * 8:(k4 + 1) * 8, cq * 256:(cq + 1) * 256]), writes=[wg[wb]])
                        P.dma(WQ[0], lambda e, wb=wb, k4=k4, cq=cq: e.dma_start(
                            out=wv[wb][:, k4 * 8:(k4 + 1) * 8, :],
                            in_=w_up[:, k4 * 8:(k4 + 1) * 8, DFF + cq * 256:DFF + (cq + 1) * 256]), writes=[wv[wb]])
                s = c % 2
                for kc in range(16):
                    P.op("tensor", lambda e, s=s, wb=wb, kc=kc, j=j: e.matmul(
                        pg[s][:, :], lhsT=wg[wb][:, kc, j * 128:(j + 1) * 128], rhs=xT[:, kc, :],
                        start=(kc == 0), stop=(kc == 15)), reads=[wg[wb], xT], writes=[pg[s]])
                for kc in range(16):
                    P.op("tensor", lambda e, s=s, wb=wb, kc=kc, j=j: e.matmul(
                        pv[s][:, :], lhsT=wv[wb][:, kc, j * 128:(j + 1) * 128], rhs=xT[:, kc, :],
                        start=(kc == 0), stop=(kc == 15)), reads=[wv[wb], xT], writes=[pv[s]])
                P.op("scalar", lambda e, s=s: e.activation(out=ug[s][:, 2:514], in_=pg[s][:, :], func=AF.Copy),
                     reads=[pg[s]], writes=[ug[s]])
                P.op("scalar", lambda e, s=s: e.activation(out=uv[s][:, 2:514], in_=pv[s][:, :], func=AF.Copy),
                     reads=[pv[s]], writes=[uv[s]])
                P.op("gpsimd", lambda e, s=s, c=c: e.tensor_copy(out=ug[s][:, 0:2], in_=carry[:, c, :]),
                     reads=[carry], writes=[ug[s]])
                P.op("gpsimd", lambda e, s=s, c=c: e.tensor_copy(out=uv[s][:, 0:2], in_=carry[:, NFC + c, :]),
                     reads=[carry], writes=[uv[s]])
                P.op("gpsimd", lambda e, s=s, c=c: e.tensor_copy(out=carry[:, c, :], in_=ug[s][:, 512:514]),
                     reads=[ug[s]], writes=[carry])
                P.op("gpsimd", lambda e, s=s, c=c: e.tensor_copy(out=carry[:, NFC + c, :], in_=uv[s][:, 512:514]),
                     reads=[uv[s]], writes=[carry])
                for (u, a, pp, cc) in ((ug[s], ag[s], pg[s], c), (uv[s], av[s], pv[s], NFC + c)):
                    P.op("scalar", lambda e, a=a, pp=pp, cc=cc: e.activation(
                        out=a[:, :], in_=pp[:, :], func=AF.Identity,
                        bias=cwb[:, 3, cc:cc + 1], scale=cwb[:, 2, cc:cc + 1]), reads=[pp, cwb], writes=[a])
                    P.op("vector", lambda e, u=u, a=a, cc=cc: e.scalar_tensor_tensor(
                        out=a[:, :], in0=u[:, 1:513], scalar=cwb[:, 1, cc:cc + 1], in1=a[:, :],
                        op0=ALU.mult, op1=ALU.add), reads=[u, cwb, a], writes=[a])
                    P.op("vector", lambda e, u=u, a=a, cc=cc: e.scalar_tensor_tensor(
                        out=a[:, :], in0=u[:, 0:512], scalar=cwb[:, 0, cc:cc + 1], in1=a[:, :],
                        op0=ALU.mult, op1=ALU.add), reads=[u, cwb, a], writes=[a])
                P.op("scalar", lambda e, s=s: e.activation(out=ag[s][:, :], in_=ag[s][:, :], func=AF.Silu),
                     reads=[ag[s]], writes=[ag[s]])
                P.op("vector", lambda e, s=s, c=c: e.tensor_tensor(
                    out=hT[:, c, :], in0=ag[s][:, :], in1=av[s][:, :], op=ALU.mult),
                    reads=[ag[s], av[s]], writes=[hT])
            if C.dbg and g == 0:
                P.dma("sync", lambda e: e.dma_start(out=C.dbg["xT"], in_=xT[:, :, :]), reads=[xT])
                P.dma("sync", lambda e: e.dma_start(out=C.dbg["hT"], in_=hT[:, :, :]), reads=[hT])
            for fg in range(4):
                for c4 in range(NFC // 4):
                    db = dcount % 3
                    dcount += 1
                    P.dma(WQ[0], lambda e, db=db, c4=c4, fg=fg: e.dma_start(
                        out=wd[db][:, :, :], in_=w_dn[:, c4 * 4:(c4 + 1) * 4, fg * 512:(fg + 1) * 512]),
                        writes=[wd[db]])
                    for i in range(4):
                        c = c4 * 4 + i
                        for tt in range(4):
                            P.op("tensor", lambda e, db=db, i=i, c=c, tt=tt: e.matmul(
                                pd[tt][:, :], lhsT=hT[:, c, tt * 128:(tt + 1) * 128], rhs=wd[db][:, i, :],
                                start=(c == 0), stop=(c == NFC - 1)), reads=[hT, wd[db]], writes=[pd[tt]])
                for tt in range(4):
                    P.op("vector", lambda e, tt=tt, fg=fg: e.scalar_tensor_tensor(
                        out=xs[tt][:, fg * 512:(fg + 1) * 512], in0=xs[tt][:, fg * 512:(fg + 1) * 512],
                        scalar=ALPHA, in1=pd[tt][:, :], op0=ALU.mult, op1=ALU.add),
                        reads=[xs[tt], pd[tt]], writes=[xs[tt]])
            if C.dbg and g == 0:
                P.dma("sync", lambda e: e.dma_start(out=C.dbg["z"], in_=xs[0][:, :]), reads=[xs[0]])
            for tt in range(4):
                resid_ln(P, C, xs[tt], lnG, lnB, sm)
                r0 = g * 512 + tt * 128
                P.dma("sync", lambda e, tt=tt, r0=r0: e.dma_start(out=Xout[r0:r0 + 128, :], in_=xs[tt][:, :]),
                      reads=[xs[tt]], writes=[P.dbuf((xout_key, g))])
    P.barrier()


def precast_phase(P, C, pairs):
    nc = P.nc
    _pfx = uq("t")
    CH = 8192
    with ExitStack() as es:
        stg = [Tl(es.enter_context(nc.sbuf_tensor(_pfx + "stg%d" % i, [128, CH], F32))) for i in range(2)]
        stb = [Tl(es.enter_context(nc.sbuf_tensor(_pfx + "stb%d" % i, [128, CH], BF16))) for i in range(2)]
        i = 0
        for src, dst in pairs:
            R, Cc = src.shape
            per = (R // 128) * Cc
            sv = src.rearrange("(p r) c -> p (r c)", p=128)
            dv = dst.rearrange("(p r) c -> p (r c)", p=128)
            for o in range(0, per, CH):
                n = min(CH, per - o)
                b = i % 2
                P.dma("sync", lambda e, b=b, o=o, n=n, sv=sv: e.dma_start(out=stg[b][:, :n], in_=sv[:, o:o + n]),
                      writes=[stg[b]])
                if i % 2:
                    P.op("scalar", lambda e, b=b, n=n: e.activation(out=stb[b][:, :n], in_=stg[b][:, :n], func=AF.Copy),
                         reads=[stg[b]], writes=[stb[b]])
                else:
                    P.op("vector", lambda e, b=b, n=n: e.tensor_copy(out=stb[b][:, :n], in_=stg[b][:, :n]),
                         reads=[stg[b]], writes=[stb[b]])
                P.dma("scalar", lambda e, b=b, o=o, n=n, dv=dv: e.dma_start(out=dv[:, o:o + n], in_=stb[b][:, :n]),
                      reads=[stb[b]])
                i += 1
    P.barrier()


class PSPool:
    def __init__(self, nc, es, prefix, n=8):
        self.t = [Tl(es.enter_context(nc.psum_tensor("%s_ps%d" % (prefix, i), [128, 512], F32))) for i in range(n)]
        self.i = 0

    def next(self):
        t = self.t[self.i % len(self.t)]
        self.i += 1
        return t


def outproj_phase(P, C, srcT, src_key, w_out, l, Xin, xin_key, Xout, xout_key, T):
    nc = P.nc
    _pfx = uq("t")
    wv_ = w_out.rearrange("(c p) n -> p c n", p=128)
    sv = srcT.rearrange("(c p) t -> p c t", p=128)
    with ExitStack() as es:
        def A(name, shape, dt):
            return Tl(es.enter_context(nc.sbuf_tensor(_pfx + name, shape, dt)))
        xs = [A("xs%d" % i, [128, D], F32) for i in range(4)]
        hT = A("hT", [128, 16, 512], BF16)
        wd = [A("wd%d" % i, [128, 4, 512], BF16) for i in range(3)]
        lnG = A("lnG", [128, D], F32)
        lnB = A("lnB", [128, D], F32)
        sm = (A("junk", [128, D], F32), A("mv", [128, 2], F32), A("rs", [128, 1], F32), A("nmr", [128, 1], F32))
        pp = PSPool(nc, es, _pfx, 8)
        P.dma("sync", lambda e: e.dma_start(out=lnG[:, :], in_=C.ln_g[l, 0]), writes=[lnG])
        P.dma("sync", lambda e: e.dma_start(out=lnB[:, :], in_=C.ln_b[l, 0]), writes=[lnB])
        dcount = 0
        for g in range(T // 512):
            for tt in range(4):
                r0 = g * 512 + tt * 128
                P.dma("sync", lambda e, tt=tt, r0=r0: e.dma_start(out=xs[tt][:, :], in_=Xin[r0:r0 + 128, :]),
                      reads=[P.dbuf((xin_key, g))], writes=[xs[tt]])
            for k4 in range(4):
                P.dma("sync", lambda e, k4=k4, g=g: e.dma_start(
                    out=hT[:, k4 * 4:(k4 + 1) * 4, :], in_=sv[:, k4 * 4:(k4 + 1) * 4, g * 512:(g + 1) * 512]),
                    reads=[P.dbuf((src_key, g))], writes=[hT])
            for fg in range(4):
                pd = [pp.next() for _ in range(4)]
                for c4 in range(4):
                    db = dcount % 3
                    dcount += 1
                    P.dma(WQ[0], lambda e, db=db, c4=c4, fg=fg: e.dma_start(
                        out=wd[db][:, :, :], in_=wv_[:, c4 * 4:(c4 + 1) * 4, fg * 512:(fg + 1) * 512]),
                        writes=[wd[db]])
                    for i in range(4):
                        c = c4 * 4 + i
                        for tt in range(4):
                            P.op("tensor", lambda e, db=db, i=i, c=c, tt=tt, pd=pd: e.matmul(
                                pd[tt][:, :], lhsT=hT[:, c, tt * 128:(tt + 1) * 128], rhs=wd[db][:, i, :],
                                start=(c == 0), stop=(c == 15)), reads=[hT, wd[db]], writes=[pd[tt]])
                for tt in range(4):
                    P.op("vector", lambda e, tt=tt, fg=fg, pd=pd: e.scalar_tensor_tensor(
                        out=xs[tt][:, fg * 512:(fg + 1) * 512], in0=xs[tt][:, fg * 512:(fg + 1) * 512],
                        scalar=ALPHA, in1=pd[tt][:, :], op0=ALU.mult, op1=ALU.add),
                        reads=[xs[tt], pd[tt]], writes=[xs[tt]])
            for tt in range(4):
                resid_ln(P, C, xs[tt], lnG, lnB, sm)
                r0 = g * 512 + tt * 128
                P.dma("sync", lambda e, tt=tt, r0=r0: e.dma_start(out=Xout[r0:r0 + 128, :], in_=xs[tt][:, :]),
                      reads=[xs[tt]], writes=[P.dbuf((xout_key, g))])
    P.barrier()


RW = 1024
C_R, C_K, C_V, C_DW, C_DA, C_DG = 0, 1024, 2048, 3072, 3136, 3200
C_GB, C_GC, C_H = 3360, 4384, 5408
PP_MUR, PP_MUK, PP_MUV, PP_W0, PP_A0, PP_KK, PP_KA, PP_LG, PP_LB, PP_V0 = [16 * i for i in range(10)]
PP_MUDW, PP_MUDA, PP_MUDG = 160, 161, 162
NPP = 168
GN_EPS = 64e-5
NEG_EHALF = -float(np.exp(-0.5))


def rwkv_prep_phase(P, C, e, l, Xin, xin_key, T):
    nc = P.nc
    _pfx = uq("t")
    NCH = T // 64
    w_in = C.ev_w_in[e].rearrange("(kc p) n -> p kc n", p=128)
    S = C.scr
    with ExitStack() as es:
        def A(name, shape, dt=F32):
            return Tl(es.enter_context(nc.sbuf_tensor(_pfx + name, shape, dt)))
        xs = [A("xs%d" % i, [128, D]) for i in range(1)]
        xT = A("xT", [128, 16, 512], BF16)
        wl = A("wl", [128, 16, 288], BF16)
        wq = [[A("wq%d_%d" % (q, i), [128, 16, 256], BF16) for i in range(2)] for q in range(3)]
        pp_ = A("pp", [64, NPP])
        w2b = A("w2b", [64, 1024], BF16)
        a2b = A("a2b", [64, 1024], BF16)
        g2b = A("g2b", [64, 3, 1024], BF16)
        rkm = A("rkm", [64, 16, 64])
        ones64 = A("ones64", [64, 64])
        rst = A("rst", [64, 512])
        cw = A("cw", [128, 3, 8])
        if e > 0:
            v1s = A("v1s", [64, 16, 32])
            v2s = A("v2s", [32, 1024])
            vv1s = A("vv1s", [32, 512])
        carry = A("carry", [64, 56])
        ccarry = A("ccarry", [128, 8, 2])
        ub = [A("ub%d" % i, [64, 513]) for i in range(2)]
        dd = [A("dd%d" % i, [64, 512]) for i in range(2)]
        tdw = A("tdw", [64, 512], BF16)
        tda = A("tda", [64, 512], BF16)
        sdg = A("sdg", [64, 3, 512], BF16)
        vall = A("vall", [64, 16, 512])
        names = ["rm", "km", "lw", "ag", "gt", "kk", "sq", "rn", "kkn", "kp", "bv",
                 "cl", "cle", "e1", "e2", "bt", "kt"]
        W = {n: A(n, [64, 512]) for n in names}
        W["vf"] = W["kk"]
        W["sv"] = W["kkn"]
        W["rkp"] = W["sq"]
        W["t1"] = W["rn"]
        W["bon"] = W["rn"]
        W["e3"] = W["cle"]
        lmix = W["sq"]
        ar = A("ar", [64, 8, 2, 64])
        wcs = A("wcs", [64, 8])
        tk = [A("tk%d" % i, [64, 8, 64]) for i in range(3)]
        hs_ = A("hs", [128, 512])
        uc = A("uc", [128, 514])
        acc = A("acc", [128, 512])
        ycb = A("ycb", [128, 512], BF16)
        pp = PSPool(nc, es, _pfx, 8)
        ident = C.ident

        P.dma("sync", lambda e_: e_.dma_start(out=pp_[:, :], in_=C.ev_pp[e]), writes=[pp_])
        P.dma(CQ[0], lambda e_: e_.dma_start(out=w2b[:, :], in_=C.ev_w2[e]), writes=[w2b])
        P.dma(CQ[0], lambda e_: e_.dma_start(out=a2b[:, :], in_=C.ev_a2[e]), writes=[a2b])
        P.dma(CQ[0], lambda e_: e_.dma_start(out=g2b[:, :, :], in_=C.ev_g2[e]), writes=[g2b])
        P.dma("sync", lambda e_: e_.dma_start(out=rkm[:, :, :], in_=C.ev_rkmat[e]), writes=[rkm])
        P.dma("sync", lambda e_: e_.dma_start(out=ones64[:, :], in_=C.c_ones64), writes=[ones64])
        P.dma("sync", lambda e_: e_.dma_start(out=rst[:, :], in_=C.c_rst), writes=[rst])
        P.dma("sync", lambda e_: e_.dma_start(out=cw[:, :, :], in_=C.ev_cw[e]), writes=[cw])
        if e > 0:
            P.dma("sync", lambda e_: e_.dma_start(out=v1s[:, :, :], in_=C.ev_v1[e - 1]), writes=[v1s])
            P.dma("sync", lambda e_: e_.dma_start(out=v2s[:, :], in_=C.ev_v2[e - 1]), writes=[v2s])
        P.op("gpsimd", lambda e_: e_.memset(carry[:, :], 0.0), writes=[carry])
        P.op("gpsimd", lambda e_: e_.memset(ccarry[:, :, :], 0.0), writes=[ccarry])

        mixn = [0]

        def proj(dst_ps, wt, c0, M):
            for kc in range(16):
                P.op("tensor", lambda e_, kc=kc: e_.matmul(
                    dst_ps[:M, :], lhsT=wt[:, kc, c0:c0 + M], rhs=xT[:, kc, :],
                    start=(kc == 0), stop=(kc == 15)), reads=[wt, xT], writes=[dst_ps])

        def mix(ps, M, mucol, cidx, out_t, out_ap_fn):
            i = mixn[0] % 2
            mixn[0] += 1
            u, d = ub[i], dd[i]
            P.op("scalar", lambda e_: e_.activation(out=u[:M, 1:513], in_=ps[:M, :], func=AF.Copy),
                 reads=[ps], writes=[u])
            P.op("gpsimd", lambda e_: e_.tensor_copy(out=u[:M, 0:1], in_=carry[:M, cidx:cidx + 1]),
                 reads=[carry], writes=[u])
            P.op("gpsimd", lambda e_: e_.tensor_copy(out=carry[:M, cidx:cidx + 1], in_=u[:M, 512:513]),
                 reads=[u], writes=[carry])
            P.op("vector", lambda e_: e_.tensor_tensor(out=d[:M, :], in0=u[:M, 0:512], in1=u[:M, 1:513],
                                                       op=ALU.subtract), reads=[u], writes=[d])
            P.op("vector", lambda e_: e_.scalar_tensor_tensor(
                out=out_ap_fn(), in0=d[:M, :], scalar=pp_[:M, mucol:mucol + 1], in1=u[:M, 1:513],
                op0=ALU.mult, op1=ALU.add), reads=[d, u, pp_], writes=[out_t])

        def tt_(eng, out_t, a, b, op, reads):
            P.op(eng, lambda e_: e_.tensor_tensor(out=out_t[:, :], in0=a, in1=b, op=op), reads=reads, writes=[out_t])

        wcnt = 0
        for g in range(T // 512):
            c0g = g * 512
            load_xT(P, C, Xin, xin_key, g, xs, xT, [pp.next() for _ in range(4)])
            for k4 in range(4):
                P.dma(WQ[0], lambda e_, k4=k4: e_.dma_start(
                    out=wl[:, k4 * 4:(k4 + 1) * 4, :], in_=w_in[:, k4 * 4:(k4 + 1) * 4, C_DW:C_DW + 288]), writes=[wl])
            for ci, (c0, M, mucol) in enumerate(((0, 64, PP_MUDW), (64, 64, PP_MUDA), (128, 64, PP_MUDG),
                                                 (192, 64, PP_MUDG + 1), (256, 32, PP_MUDG + 2))):
                ps = pp.next()
                proj(ps, wl, c0, M)
                mix(ps, M, mucol, 48 + ci, lmix, lambda M=M: lmix[:M, :])
                if ci == 0:
                    P.op("scalar", lambda e_: e_.activation(out=tdw[:, :], in_=lmix[:, :], func=AF.Tanh),
                         reads=[lmix], writes=[tdw])
                elif ci == 1:
                    P.op("scalar", lambda e_: e_.activation(out=tda[:, :], in_=lmix[:, :], func=AF.Copy),
                         reads=[lmix], writes=[tda])
                else:
                    P.op("scalar", lambda e_, M=M, q=ci - 2: e_.activation(
                        out=sdg[:M, q, :], in_=lmix[:M, :], func=AF.Sigmoid), reads=[lmix], writes=[sdg])
            for h in range(16):
                hq, hh = divmod(h, 4)
                if hh == 0:
                    wb = wcnt % 2
                    wcnt += 1
                    for k4 in range(2):
                        P.dma(WQ[0], lambda e_, k4=k4, wb=wb, hq=hq: e_.dma_start(
                            out=wq[2][wb][:, k4 * 8:(k4 + 1) * 8, :],
                            in_=w_in[:, k4 * 8:(k4 + 1) * 8, C_V + hq * 256:C_V + (hq + 1) * 256]), writes=[wq[2][wb]])
                ps = pp.next()
                proj(ps, wq[2][wb], hh * 64, 64)
                mix(ps, 64, PP_MUV + h, 32 + h, vall, lambda h=h: vall[:, h, :])
            if e == 0:
                for h in range(16):
                    P.dma("sync", lambda e_, h=h, c0g=c0g: e_.dma_start(out=S["vfirst"][h, :, c0g:c0g + 512], in_=vall[:, h, :]),
                          reads=[vall], writes=[P.dbuf(("vfirst", g))])
            else:
                ps = pp.next()
                for h in range(16):
                    P.op("tensor", lambda e_, h=h: e_.matmul(ps[:32, :], lhsT=v1s[:, h, :], rhs=vall[:, h, :],
                                                            start=(h == 0), stop=(h == 15)),
                         reads=[v1s, vall], writes=[ps])
                P.op("scalar", lambda e_: e_.activation(out=vv1s[:, :], in_=ps[:32, :], func=AF.Copy),
                     reads=[ps], writes=[vv1s])
                for h in range(16):
                    ps2 = pp.next()
                    P.op("tensor", lambda e_, h=h, ps2=ps2: e_.matmul(
                        ps2[:64, :], lhsT=v2s[:, h * 64:(h + 1) * 64], rhs=vv1s[:, :], start=True, stop=True),
                        reads=[v2s, vv1s], writes=[ps2])
                    P.op("scalar", lambda e_, h=h, ps2=ps2: e_.activation(
                        out=W["sv"][:, :], in_=ps2[:64, :], func=AF.Sigmoid, bias=pp_[:, PP_V0 + h:PP_V0 + h + 1]),
                        reads=[ps2, pp_], writes=[W["sv"]])
                    P.dma("sync", lambda e_, h=h, c0g=c0g: e_.dma_start(out=W["vf"][:, :], in_=S["vfirst"][h, :, c0g:c0g + 512]),
                          reads=[P.dbuf(("vfirst", g))], writes=[W["vf"]])
                    tt_("vector", W["vf"], W["vf"][:, :], vall[:, h, :], ALU.subtract, [W["vf"], vall])
                    tt_("vector", W["vf"], W["vf"][:, :], W["sv"][:, :], ALU.mult, [W["vf"], W["sv"]])
                    P.op("vector", lambda e_, h=h: e_.tensor_tensor(
                        out=vall[:, h, :], in0=vall[:, h, :], in1=W["vf"][:, :], op=ALU.add),
                        reads=[vall, W["vf"]], writes=[vall])
            for h in range(16):
                hq, hh = divmod(h, 4)
                if hh == 0:
                    wb = wcnt % 2
                    wcnt += 1
                    for q, cbase in ((0, C_R), (1, C_K)):
                        for k4 in range(2):
                            P.dma(WQ[0], lambda e_, k4=k4, wb=wb, hq=hq, q=q, cbase=cbase: e_.dma_start(
                                out=wq[q][wb][:, k4 * 8:(k4 + 1) * 8, :],
                                in_=w_in[:, k4 * 8:(k4 + 1) * 8, cbase + hq * 256:cbase + (hq + 1) * 256]),
                                writes=[wq[q][wb]])
                hc = slice(h * 64, (h + 1) * 64)
                ps = pp.next()
                proj(ps, wq[0][wb], hh * 64, 64)
                mix(ps, 64, PP_MUR + h, h, W["rm"], lambda: W["rm"][:, :])
                ps = pp.next()
                proj(ps, wq[1][wb], hh * 64, 64)
                mix(ps, 64, PP_MUK + h, 16 + h, W["km"], lambda: W["km"][:, :])
                ps = pp.next()
                P.op("tensor", lambda e_, ps=ps, hc=hc: e_.matmul(ps[:64, :], lhsT=w2b[:, hc], rhs=tdw[:, :],
                                                                 start=True, stop=True), reads=[w2b, tdw], writes=[ps])
                P.op("scalar", lambda e_, ps=ps, h=h: e_.activation(
                    out=W["lw"][:, :], in_=ps[:64, :], func=AF.Sigmoid, bias=pp_[:, PP_W0 + h:PP_W0 + h + 1]),
                    reads=[ps, pp_], writes=[W["lw"]])
                P.op("vector", lambda e_: e_.tensor_scalar(out=W["lw"][:, :], in0=W["lw"][:, :], scalar1=NEG_EHALF,
                                                          scalar2=None, op0=ALU.mult), reads=[W["lw"]], writes=[W["lw"]])
                ps = pp.next()
                P.op("tensor", lambda e_, ps=ps, hc=hc: e_.matmul(ps[:64, :], lhsT=a2b[:, hc], rhs=tda[:, :],
                                                                 start=True, stop=True), reads=[a2b, tda], writes=[ps])
                P.op("scalar", lambda e_, ps=ps, h=h: e_.activation(
                    out=W["ag"][:, :], in_=ps[:64, :], func=AF.Sigmoid, bias=pp_[:, PP_A0 + h:PP_A0 + h + 1]),
                    reads=[ps, pp_], writes=[W["ag"]])
                ps = pp.next()
                for q, K in ((0, 64), (1, 64), (2, 32)):
                    P.op("tensor", lambda e_, ps=ps, hc=hc, q=q, K=K: e_.matmul(
                        ps[:64, :], lhsT=g2b[:K, q, hc], rhs=sdg[:K, q, :], start=(q == 0), stop=(q == 2)),
                        reads=[g2b, sdg], writes=[ps])
                P.op("scalar", lambda e_, ps=ps: e_.activation(out=W["gt"][:, :], in_=ps[:64, :], func=AF.Copy),
                     reads=[ps], writes=[W["gt"]])
                P.dma("sync", lambda e_, h=h, c0g=c0g: e_.dma_start(out=S["gT"][h, :, c0g:c0g + 512], in_=W["gt"][:, :]),
                      reads=[W["gt"]], writes=[P.dbuf(("gT", g))])
                P.op("vector", lambda e_, h=h: e_.tensor_scalar(
                    out=W["kk"][:, :], in0=W["km"][:, :], scalar1=pp_[:, PP_KK + h:PP_KK + h + 1], scalar2=None,
                    op0=ALU.mult), reads=[W["km"], pp_], writes=[W["kk"]])
                tt_("gpsimd", W["sq"], W["kk"][:, :], W["kk"][:, :], ALU.mult, [W["kk"]])
                ps = pp.next()
                P.op("tensor", lambda e_, ps=ps: e_.matmul(ps[:64, :], lhsT=ones64[:, :], rhs=W["sq"][:, :],
                                                          start=True, stop=True), reads=[ones64, W["sq"]], writes=[ps])
                P.op("vector", lambda e_, ps=ps: e_.tensor_scalar(out=W["rn"][:, :], in0=ps[:64, :], scalar1=1e-24,
                                                                 scalar2=None, op0=ALU.max), reads=[ps], writes=[W["rn"]])
                P.op("scalar", lambda e_: e_.activation(out=W["rn"][:, :], in_=W["rn"][:, :], func=AF.Sqrt),
                     reads=[W["rn"]], writes=[W["rn"]])
                P.op("vector", lambda e_: e_.reciprocal(out=W["rn"][:, :], in_=W["rn"][:, :]),
                     reads=[W["rn"]], writes=[W["rn"]])
                tt_("vector", W["kkn"], W["kk"][:, :], W["rn"][:, :], ALU.mult, [W["kk"], W["rn"]])
                P.op("vector", lambda e_, h=h: e_.tensor_scalar(
                    out=W["t1"][:, :], in0=W["ag"][:, :], scalar1=pp_[:, PP_KA + h:PP_KA + h + 1],
                    scalar2=pp_[:, PP_KA + h:PP_KA + h + 1], op0=ALU.mult, op1=ALU.subtract),
                    reads=[W["ag"], pp_], writes=[W["t1"]])
                P.op("vector", lambda e_: e_.scalar_tensor_tensor(
                    out=W["kp"][:, :], in0=W["t1"][:, :], scalar=1.0, in1=W["km"][:, :], op0=ALU.add, op1=ALU.mult),
                    reads=[W["t1"], W["km"]], writes=[W["kp"]])
                tt_("gpsimd", W["bv"], W["kkn"][:, :], W["ag"][:, :], ALU.mult, [W["kkn"], W["ag"]])
                tt_("gpsimd", W["rkp"], W["rm"][:, :], W["kp"][:, :], ALU.mult, [W["rm"], W["kp"]])
                ps = pp.next()
                P.op("tensor", lambda e_, ps=ps, h=h: e_.matmul(ps[:64, :], lhsT=rkm[:, h, :], rhs=W["rkp"][:, :],
                                                               start=True, stop=True), reads=[rkm, W["rkp"]], writes=[ps])
                P.op("vector", lambda e_, ps=ps, h=h: e_.tensor_tensor(
                    out=W["bon"][:, :], in0=ps[:64, :], in1=vall[:, h, :], op=ALU.mult),
                    reads=[ps, vall], writes=[W["bon"]])
                P.dma("sync", lambda e_, h=h, c0g=c0g: e_.dma_start(out=S["bonT"][h, :, c0g:c0g + 512], in_=W["bon"][:, :]),
                      reads=[W["bon"]], writes=[P.dbuf(("bonT", g))])
                P.op("vector", lambda e_: e_.tensor_tensor_scan(
                    out=W["cl"][:, :], data0=rst[:, :], data1=W["lw"][:, :], initial=0.0, op0=ALU.mult, op1=ALU.add),
                    reads=[rst, W["lw"]], writes=[W["cl"]])
                tt_("gpsimd", W["cle"], W["cl"][:, :], W["lw"][:, :], ALU.subtract, [W["cl"], W["lw"]])
                P.op("scalar", lambda e_: e_.activation(out=W["e1"][:, :], in_=W["cl"][:, :], func=AF.Exp),
                     reads=[W["cl"]], writes=[W["e1"]])
                P.op("scalar", lambda e_: e_.activation(out=W["e2"][:, :], in_=W["cl"][:, :], func=AF.Exp, scale=-1.0),
                     reads=[W["cl"]], writes=[W["e2"]])
                P.op("scalar", lambda e_: e_.activation(out=W["e3"][:, :], in_=W["cle"][:, :], func=AF.Exp),
                     reads=[W["cle"]], writes=[W["e3"]])
                P.op("vector", lambda e_: e_.scalar_tensor_tensor(
                    out=ar[:, :, 0, :], in0=W["kkn"][:, :].rearrange("p (c t) -> p c t", c=8), scalar=-1.0,
                    in1=W["e3"][:, :].rearrange("p (c t) -> p c t", c=8), op0=ALU.mult, op1=ALU.mult),
                    reads=[W["kkn"], W["e3"]], writes=[ar])
                P.op("gpsimd", lambda e_: e_.tensor_tensor(
                    out=ar[:, :, 1, :], in0=W["rm"][:, :].rearrange("p (c t) -> p c t", c=8),
                    in1=W["e1"][:, :].rearrange("p (c t) -> p c t", c=8), op=ALU.mult),
                    reads=[W["rm"], W["e1"]], writes=[ar])
                P.dma("sync", lambda e_, h=h, g=g: e_.dma_start(out=S["ar"][h, :, g * 8:(g + 1) * 8, :, :], in_=ar[:, :, :, :]),
                      reads=[ar], writes=[P.dbuf(("ar", g))])
                tt_("vector", W["bt"], W["bv"][:, :], W["e2"][:, :], ALU.mult, [W["bv"], W["e2"]])
                tt_("gpsimd", W["kt"], W["kp"][:, :], W["e2"][:, :], ALU.mult, [W["kp"], W["e2"]])
                P.dma("sync", lambda e_, h=h, c0g=c0g: e_.dma_start(out=S["bt"][h, :, c0g:c0g + 512], in_=W["bt"][:, :]),
                      reads=[W["bt"]], writes=[P.dbuf(("bt", g))])
                P.dma("sync", lambda e_, h=h, c0g=c0g: e_.dma_start(out=S["kt"][h, :, c0g:c0g + 512], in_=W["kt"][:, :]),
                      reads=[W["kt"]], writes=[P.dbuf(("kt", g))])
                P.op("scalar", lambda e_: e_.activation(
                    out=wcs[:, :], in_=W["e1"][:, :].rearrange("p (c t) -> p c t", c=8)[:, :, 63], func=AF.Copy),
                    reads=[W["e1"]], writes=[wcs])
                P.dma("sync", lambda e_, h=h, g=g: e_.dma_start(out=S["wc"][h, :, g * 8:(g + 1) * 8], in_=wcs[:, :]),
                      reads=[wcs], writes=[P.dbuf(("wc", g))])
                for q, (src, sap) in enumerate(((W["bt"], lambda ch: W["bt"][:, ch * 64:(ch + 1) * 64]),
                                                (W["kt"], lambda ch: W["kt"][:, ch * 64:(ch + 1) * 64]),
                                                (vall, lambda ch, h=h: vall[:, h, ch * 64:(ch + 1) * 64]))):
                    ps = pp.next()
                    for ch in range(8):
                        P.op("tensor", lambda e_, ps=ps, ch=ch, sap=sap: e_.transpose(
                            ps[:64, ch * 64:(ch + 1) * 64], sap(ch), ident[:64, :64]),
                            reads=[src, ident], writes=[ps])
                    P.op("scalar" if q != 1 else "vector",
                         (lambda e_, ps=ps, q=q: e_.activation(
                             out=tk[q][:, :, :], in_=ps[:64, :].rearrange("p (c j) -> p c j", c=8), func=AF.Copy))
                         if q != 1 else
                         (lambda e_, ps=ps, q=q: e_.tensor_copy(
                             out=tk[q][:, :, :], in_=ps[:64, :].rearrange("p (c j) -> p c j", c=8))),
                         reads=[ps], writes=[tk[q]])
                    P.dma("sync", lambda e_, h=h, g=g, q=q: e_.dma_start(
                        out=S["tok%d" % q][h, :, g * 8:(g + 1) * 8, :], in_=tk[q][:, :, :]),
                        reads=[tk[q]], writes=[P.dbuf(("tok%d" % q, g))])
            for cc in range(8):
                cp, ci2 = divmod(cc, 2)
                if ci2 == 0:
                    wb = wcnt % 2
                    wcnt += 1
                    for q, cbase in ((0, C_GB), (1, C_GC), (2, C_H)):
                        for k4 in range(2):
                            P.dma(WQ[0], lambda e_, k4=k4, wb=wb, cp=cp, q=q, cbase=cbase: e_.dma_start(
                                out=wq[q][wb][:, k4 * 8:(k4 + 1) * 8, :],
                                in_=w_in[:, k4 * 8:(k4 + 1) * 8, cbase + cp * 256:cbase + (cp + 1) * 256]),
                                writes=[wq[q][wb]])
                pgb, pgc, ph = pp.next(), pp.next(), pp.next()
                proj(pgb, wq[0][wb], ci2 * 128, 128)
                proj(pgc, wq[1][wb], ci2 * 128, 128)
                proj(ph, wq[2][wb], ci2 * 128, 128)
                P.op("scalar", lambda e_, ph=ph: e_.activation(out=hs_[:, :], in_=ph[:, :], func=AF.Copy),
                     reads=[ph], writes=[hs_])
                P.op("vector", lambda e_, pgc=pgc: e_.tensor_tensor(out=uc[:, 2:514], in0=pgc[:, :], in1=hs_[:, :],
                                                                   op=ALU.mult), reads=[pgc, hs_], writes=[uc])
                P.op("gpsimd", lambda e_, cc=cc: e_.tensor_copy(out=uc[:, 0:2], in_=ccarry[:, cc, :]),
                     reads=[ccarry], writes=[uc])
                P.op("gpsimd", lambda e_, cc=cc: e_.tensor_copy(out=ccarry[:, cc, :], in_=uc[:, 512:514]),
                     reads=[uc], writes=[ccarry])
                P.op("vector", lambda e_, cc=cc: e_.tensor_scalar(
                    out=acc[:, :], in0=uc[:, 2:514], scalar1=cw[:, 2, cc:cc + 1], scalar2=None, op0=ALU.mult),
                    reads=[uc, cw], writes=[acc])
                for k in (1, 0):
                    P.op("vector", lambda e_, cc=cc, k=k: e_.scalar_tensor_tensor(
                        out=acc[:, :], in0=uc[:, k:k + 512], scalar=cw[:, k, cc:cc + 1], in1=acc[:, :],
                        op0=ALU.mult, op1=ALU.add), reads=[uc, cw, acc], writes=[acc])
                P.op("vector", lambda e_, pgb=pgb: e_.tensor_tensor(out=ycb[:, :], in0=pgb[:, :], in1=acc[:, :],
                                                                   op=ALU.mult), reads=[pgb, acc], writes=[ycb])
                P.dma("sync", lambda e_, cc=cc, c0g=c0g: e_.dma_start(
                    out=S["catT"][RW + cc * 128:RW + (cc + 1) * 128, c0g:c0g + 512], in_=ycb[:, :]),
                    reads=[ycb], writes=[P.dbuf(("catT_c", g))])
    P.barrier()


def rwkv_scan_phase(P, C, e, T):
    nc = P.nc
    _pfx = uq("t")
    NCH = T // 64
    S = C.scr
    arv = S["ar"].rearrange("h j c a t -> j h c (a t)")
    btv = S["bt"].rearrange("h j t -> j h t")
    ktv = S["kt"].rearrange("h j t -> j h t")
    tokv = [S["tok%d" % q].rearrange("h t c j -> t h c j") for q in range(3)]
    gTv = S["gT"].rearrange("h i t -> i h t")
    bonv = S["bonT"].rearrange("h i t -> i h t")
    wcv = S["wc"].rearrange("h j c -> j h c")
    catv = S["catT"][0:RW, :].rearrange("(h i) t -> i h t", i=64)
    with ExitStack() as es:
        def A(name, shape, dt=F32):
            return Tl(es.enter_context(nc.sbuf_tensor(_pfx + name, shape, dt)))
        NB = 2
        arc = [A("arc%d" % i, [64, 16, 128]) for i in range(NB)]
        btc = [A("btc%d" % i, [64, 16, 64]) for i in range(NB)]
        ktc = [A("ktc%d" % i, [64, 16, 64]) for i in range(NB)]
        tkc = [[A("tkc%d_%d" % (q, i), [64, 16, 64]) for i in range(NB)] for q in range(3)]
        gtc = [A("gtc%d" % i, [64, 16, 64]) for i in range(NB)]
        bnc = [A("bnc%d" % i, [64, 16, 64]) for i in range(NB)]
        wca = A("wca", [64, 16, NCH])
        PST = [A("Pst%d" % i, [64, 8, 64]) for i in range(2)]
        TMP = []
        for i in range(2):
            TMP.append((A("ptmp%d" % i, [64, 8, 64]),
                        [A("Nm%d_%d" % (i, j), [64, 8, 64], BF16) for j in range(2)],
                        [A("NTm%d_%d" % (i, j), [64, 8, 64], BF16) for j in range(2)],
                        A("ABRB%d" % i, [64, 8, 2, 64]), A("AKRK%d" % i, [64, 8, 2, 64]),
                        A("TT%d" % i, [64, 8, 64]), A("X2%d" % i, [64, 8, 64]), A("Xs%d" % i, [64, 8, 64]),
                        A("Us%d" % i, [64, 8, 64]), A("ysb%d" % i, [64, 512]), A("dlt%d" % i, [64, 512]),
                        A("sq%d" % i, [64, 512]), A("rs%d" % i, [64, 512]), A("yo%d" % i, [64, 8, 64], BF16),
                        A("TTb%d" % i, [64, 8, 64], BF16)))
        pp_ = A("pp", [64, NPP])
        ones64 = A("ones64", [64, 64])
        maskL = A("maskL", [64, 64])
        mask2 = A("mask2", [64, 2, 64])
        id64 = A("id64", [64, 64])
        pp = PSPool(nc, es, _pfx, 8)
        P.dma("sync", lambda e_: e_.dma_start(out=pp_[:, :], in_=C.ev_pp[e]), writes=[pp_])
        P.dma("sync", lambda e_: e_.dma_start(out=ones64[:, :], in_=C.c_ones64), writes=[ones64])
        P.dma("sync", lambda e_: e_.dma_start(out=maskL[:, :], in_=C.c_maskL), writes=[maskL])
        P.dma("sync", lambda e_: e_.dma_start(out=mask2[:, :, :], in_=C.c_mask2), writes=[mask2])
        P.dma("sync", lambda e_: e_.dma_start(out=id64[:, :], in_=C.c_id64), writes=[id64])
        P.dma("sync", lambda e_: e_.dma_start(out=wca[:, :, :], in_=wcv),
              reads=[P.dbuf(("wc", g)) for g in range(T // 512)], writes=[wca])
        for i in range(2):
            P.op("gpsimd", lambda e_, i=i: e_.memset(PST[i][:, :, :], 0.0), writes=[PST[i]])

        def bc(ap2, n=64):
            return ap2.unsqueeze(2).broadcast_to([64, 8, n])

        def chunk_half(c, hf, b, g, t0):
            AR, BT, KT = arc[b], btc[b], ktc[b]
            BK, KK, VK = tkc[0][b], tkc[1][b], tkc[2][b]
            h0 = hf * 8
            Pst = PST[hf]
            ptmp, Nm, NTm, ABRB, AKRK, TT, X2, Xs, Us, ysb, dlt, sq, rs, yo, TTb = TMP[hf]
            psN, psAB0, psAB1, psAK0, psAK1 = pp.next(), pp.next(), pp.next(), pp.next(), pp.next()
            psAB = (psAB0, psAB1)
            psAK = (psAK0, psAK1)
            for hh in range(8):
                h = h0 + hh
                cs = slice(hh * 64, (hh + 1) * 64)
                c2 = slice((hh % 4) * 128, (hh % 4 + 1) * 128)
                P.op("tensor", lambda e_, h=h, cs=cs: e_.matmul(psN[:64, cs], lhsT=AR[:, h, 0:64], rhs=BT[:, h, :],
                                                               start=True, stop=True), reads=[AR, BT], writes=[psN])
                P.op("tensor", lambda e_, h=h, c2=c2, pt=psAB[hh // 4]: e_.matmul(
                    pt[:64, c2], lhsT=BT[:, h, :], rhs=AR[:, h, :], start=True, stop=True),
                    reads=[AR, BT], writes=[psAB[hh // 4]])
                P.op("tensor", lambda e_, h=h, c2=c2, pt=psAK[hh // 4]: e_.matmul(
                    pt[:64, c2], lhsT=KT[:, h, :], rhs=AR[:, h, :], start=True, stop=True),
                    reads=[AR, KT], writes=[psAK[hh // 4]])
            P.op("vector", lambda e_: e_.tensor_tensor(
                out=Nm[0][:, :, :], in0=psN[:64, :].rearrange("p (h s) -> p h s", h=8),
                in1=maskL[:, :].unsqueeze(1).broadcast_to([64, 8, 64]), op=ALU.mult),
                reads=[psN, maskL], writes=[Nm[0]])
            for q in range(2):
                P.op("vector", lambda e_, q=q: e_.tensor_tensor(
                    out=ABRB[:, q * 4:(q + 1) * 4, :, :],
                    in0=psAB[q][:64, :].rearrange("p (h a t) -> p h a t", h=4, a=2),
                    in1=mask2[:, :, :].unsqueeze(1).broadcast_to([64, 4, 2, 64]), op=ALU.mult),
                    reads=[psAB[q], mask2], writes=[ABRB])
                P.op("vector", lambda e_, q=q: e_.tensor_tensor(
                    out=AKRK[:, q * 4:(q + 1) * 4, :, :],
                    in0=psAK[q][:64, :].rearrange("p (h a t) -> p h a t", h=4, a=2),
                    in1=mask2[:, :, :].unsqueeze(1).broadcast_to([64, 4, 2, 64]), op=ALU.mult),
                    reads=[psAK[q], mask2], writes=[AKRK])
            yield
            P.op("gpsimd", lambda e_: e_.tensor_copy(out=NTm[0][:, :, :], in_=ABRB[:, :, 0, :]),
                 reads=[ABRB], writes=[NTm[0]])
            P.op("vector", lambda e_: e_.tensor_tensor(
                out=TT[:, :, :], in0=ABRB[:, :, 0, :], in1=id64[:, :].unsqueeze(1).broadcast_to([64, 8, 64]),
                op=ALU.add), reads=[ABRB, id64], writes=[TT])
            P.op("gpsimd", lambda e_: e_.tensor_copy(out=TTb[:, :, :], in_=TT[:, :, :]), reads=[TT], writes=[TTb])
            cur = 0
            for k in range(1, 6):
                nxt = 1 - cur
                psA, psB, psC = pp.next(), pp.next(), pp.next()
                for hh in range(8):
                    cs = slice(hh * 64, (hh + 1) * 64)
                    if k < 5:
                        P.op("tensor", lambda e_, hh=hh, cs=cs, cur=cur, psA=psA: e_.matmul(
                            psA[:64, cs], lhsT=Nm[cur][:, hh, :], rhs=NTm[cur][:, hh, :], start=True, stop=True),
                            reads=[Nm[cur], NTm[cur]], writes=[psA])
                    P.op("tensor", lambda e_, hh=hh, cs=cs, cur=cur, psB=psB: e_.matmul(
                        psB[:64, cs], lhsT=NTm[cur][:, hh, :], rhs=Nm[cur][:, hh, :], start=True, stop=True),
                        reads=[Nm[cur], NTm[cur]], writes=[psB])
                P.op("scalar", lambda e_, nxt=nxt, psB=psB: e_.activation(
                    out=Nm[nxt][:, :, :], in_=psB[:64, :].rearrange("p (h s) -> p h s", h=8), func=AF.Copy),
                    reads=[psB], writes=[Nm[nxt]])
                if k < 5:
                    P.op("vector", lambda e_, nxt=nxt, psA=psA: e_.tensor_copy(
                        out=NTm[nxt][:, :, :], in_=psA[:64, :].rearrange("p (h s) -> p h s", h=8)),
                        reads=[psA], writes=[NTm[nxt]])
                yield
                for hh in range(8):
                    cs = slice(hh * 64, (hh + 1) * 64)
                    P.op("tensor", lambda e_, hh=hh, cs=cs, nxt=nxt, psC=psC: e_.matmul(
                        psC[:64, cs], lhsT=Nm[nxt][:, hh, :], rhs=TTb[:, hh, :], start=True, stop=True),
                        reads=[Nm[nxt], TTb], writes=[psC])
                P.op("vector", lambda e_, psC=psC: e_.tensor_tensor(
                    out=TT[:, :, :], in0=TT[:, :, :], in1=psC[:64, :].rearrange("p (h s) -> p h s", h=8),
                    op=ALU.add), reads=[TT, psC], writes=[TT])
                if k < 5:
                    P.op("gpsimd", lambda e_: e_.tensor_copy(out=TTb[:, :, :], in_=TT[:, :, :]), reads=[TT], writes=[TTb])
                yield
                cur = nxt
            psX = pp.next()
            for hh in range(8):
                h = h0 + hh
                cs = slice(hh * 64, (hh + 1) * 64)
                P.op("tensor", lambda e_, hh=hh, h=h, cs=cs: e_.matmul(
                    psX[:64, cs], lhsT=AKRK[:, hh, 0, :], rhs=VK[:, h, :], start=True, stop=True),
                    reads=[AKRK, VK], writes=[psX])
            P.op("scalar", lambda e_: e_.activation(
                out=X2[:, :, :], in_=psX[:64, :].rearrange("p (h s) -> p h s", h=8), func=AF.Copy),
                reads=[psX], writes=[X2])
            yield
            ps1 = pp.next()
            for hh in range(8):
                h = h0 + hh
                cs = slice(hh * 64, (hh + 1) * 64)
                P.op("tensor", lambda e_, h=h, hh=hh, cs=cs: e_.matmul(
                    ps1[:64, cs], lhsT=AR[:, h, 0:64], rhs=Pst[:, hh, :], start=True, stop=True),
                    reads=[AR, Pst], writes=[ps1])
            P.op("vector", lambda e_: e_.tensor_tensor(
                out=Xs[:, :, :], in0=ps1[:64, :].rearrange("p (h s) -> p h s", h=8), in1=X2[:, :, :], op=ALU.add),
                reads=[ps1, X2], writes=[Xs])
            yield
            ps2 = pp.next()
            for hh in range(8):
                cs = slice(hh * 64, (hh + 1) * 64)
                P.op("tensor", lambda e_, hh=hh, cs=cs: e_.matmul(
                    ps2[:64, cs], lhsT=TT[:, hh, :], rhs=Xs[:, hh, :], start=True, stop=True),
                    reads=[TT, Xs], writes=[ps2])
            P.op("scalar", lambda e_: e_.activation(
                out=Us[:, :, :], in_=ps2[:64, :].rearrange("p (h s) -> p h s", h=8), func=AF.Copy),
                reads=[ps2], writes=[Us])
            yield
            psY = pp.next()
            for hh in range(8):
                h = h0 + hh
                cs = slice(hh * 64, (hh + 1) * 64)
                P.op("tensor", lambda e_, h=h, hh=hh, cs=cs: e_.matmul(
                    psY[:64, cs], lhsT=Pst[:, hh, :], rhs=AR[:, h, 64:128], start=True, stop=False),
                    reads=[AR, Pst], writes=[psY])
                P.op("tensor", lambda e_, hh=hh, cs=cs: e_.matmul(
                    psY[:64, cs], lhsT=Us[:, hh, :], rhs=ABRB[:, hh, 1, :], start=False, stop=False),
                    reads=[Us, ABRB], writes=[psY])
                P.op("tensor", lambda e_, hh=hh, h=h, cs=cs: e_.matmul(
                    psY[:64, cs], lhsT=VK[:, h, :], rhs=AKRK[:, hh, 1, :], start=False, stop=True),
                    reads=[VK, AKRK], writes=[psY])
            psP = pp.next()
            for hh in range(8):
                h = h0 + hh
                cs = slice(hh * 64, (hh + 1) * 64)
                P.op("tensor", lambda e_, hh=hh, h=h, cs=cs: e_.matmul(
                    psP[:64, cs], lhsT=BK[:, h, :], rhs=Us[:, hh, :], start=True, stop=False),
                    reads=[BK, Us], writes=[psP])
                P.op("tensor", lambda e_, h=h, cs=cs: e_.matmul(
                    psP[:64, cs], lhsT=KK[:, h, :], rhs=VK[:, h, :], start=False, stop=True),
                    reads=[KK, VK], writes=[psP])
            P.op("vector", lambda e_, h0=h0: e_.tensor_tensor(
                out=ptmp[:, :, :], in0=Pst[:, :, :], in1=psP[:64, :].rearrange("p (h s) -> p h s", h=8),
                op=ALU.add), reads=[Pst, psP], writes=[ptmp])
            P.op("vector", lambda e_, h0=h0, c=c: e_.tensor_tensor(
                out=Pst[:, :, :], in0=ptmp[:, :, :], in1=bc(wca[:, h0:h0 + 8, c]), op=ALU.mult),
                reads=[ptmp, wca], writes=[Pst])
            P.op("scalar", lambda e_: e_.activation(out=ysb[:, :], in_=psY[:64, :], func=AF.Copy),
                 reads=[psY], writes=[ysb])
            yield
            pm = pp.next()
            P.op("tensor", lambda e_, pm=pm: e_.matmul(pm[:64, :], lhsT=ones64[:, :], rhs=ysb[:, :], start=True, stop=True),
                 reads=[ones64, ysb], writes=[pm])
            P.op("vector", lambda e_, pm=pm: e_.scalar_tensor_tensor(
                out=dlt[:, :], in0=pm[:64, :], scalar=-1.0 / 64, in1=ysb[:, :], op0=ALU.mult, op1=ALU.add),
                reads=[pm, ysb], writes=[dlt])
            yield
            P.op("gpsimd", lambda e_: e_.tensor_tensor(out=sq[:, :], in0=dlt[:, :], in1=dlt[:, :], op=ALU.mult),
                 reads=[dlt], writes=[sq])
            pv = pp.next()
            P.op("tensor", lambda e_, pv=pv: e_.matmul(pv[:64, :], lhsT=ones64[:, :], rhs=sq[:, :], start=True, stop=True),
                 reads=[ones64, sq], writes=[pv])
            P.op("vector", lambda e_, pv=pv: e_.tensor_scalar(
                out=rs[:, :], in0=pv[:64, :], scalar1=1.0 / 64, scalar2=GN_EPS, op0=ALU.mult, op1=ALU.add),
                reads=[pv], writes=[rs])
            yield
            P.op("scalar", lambda e_: e_.activation(out=rs[:, :], in_=rs[:, :], func=AF.Sqrt), reads=[rs], writes=[rs])
            P.op("vector", lambda e_: e_.reciprocal(out=rs[:, :], in_=rs[:, :]), reads=[rs], writes=[rs])
            P.op("vector", lambda e_: e_.tensor_tensor(out=dlt[:, :], in0=dlt[:, :], in1=rs[:, :], op=ALU.mult),
                 reads=[dlt, rs], writes=[dlt])
            d3 = lambda: dlt[:, :].rearrange("p (h s) -> p h s", h=8)
            P.op("vector", lambda e_, h0=h0: e_.tensor_tensor(
                out=d3(), in0=d3(), in1=bc(pp_[:, PP_LG + h0:PP_LG + h0 + 8]), op=ALU.mult),
                reads=[dlt, pp_], writes=[dlt])
            P.op("vector", lambda e_, h0=h0: e_.tensor_tensor(
                out=d3(), in0=d3(), in1=bc(pp_[:, PP_LB + h0:PP_LB + h0 + 8]), op=ALU.add),
                reads=[dlt, pp_], writes=[dlt])
            P.op("gpsimd", lambda e_, h0=h0, b=b: e_.tensor_tensor(
                out=d3(), in0=d3(), in1=bnc[b][:, h0:h0 + 8, :], op=ALU.add), reads=[dlt, bnc[b]], writes=[dlt])
            P.op("vector", lambda e_, h0=h0, b=b: e_.tensor_tensor(
                out=yo[:, :, :], in0=d3(), in1=gtc[b][:, h0:h0 + 8, :], op=ALU.mult),
                reads=[dlt, gtc[b]], writes=[yo])
            P.dma("sync", lambda e_, h0=h0, t0=t0: e_.dma_start(out=catv[:, h0:h0 + 8, t0:t0 + 64], in_=yo[:, :, :]),
                  reads=[yo], writes=[P.dbuf(("catT_r", g))])

        for c in range(NCH):
            b = c % NB
            g = c // 8
            t0 = c * 64
            P.dma("sync", lambda e_, b=b, c=c: e_.dma_start(out=arc[b][:, :, :], in_=arv[:, :, c, :]),
                  reads=[P.dbuf(("ar", g))], writes=[arc[b]])
            P.dma("sync", lambda e_, b=b, t0=t0: e_.dma_start(out=btc[b][:, :, :], in_=btv[:, :, t0:t0 + 64]),
                  reads=[P.dbuf(("bt", g))], writes=[btc[b]])
            P.dma("sync", lambda e_, b=b, t0=t0: e_.dma_start(out=ktc[b][:, :, :], in_=ktv[:, :, t0:t0 + 64]),
                  reads=[P.dbuf(("kt", g))], writes=[ktc[b]])
            for q in range(3):
                P.dma("sync", lambda e_, b=b, c=c, q=q: e_.dma_start(out=tkc[q][b][:, :, :], in_=tokv[q][:, :, c, :]),
                      reads=[P.dbuf(("tok%d" % q, g))], writes=[tkc[q][b]])
            P.dma("sync", lambda e_, b=b, t0=t0: e_.dma_start(out=gtc[b][:, :, :], in_=gTv[:, :, t0:t0 + 64]),
                  reads=[P.dbuf(("gT", g))], writes=[gtc[b]])
            P.dma("sync", lambda e_, b=b, t0=t0: e_.dma_start(out=bnc[b][:, :, :], in_=bonv[:, :, t0:t0 + 64]),
                  reads=[P.dbuf(("bonT", g))], writes=[bnc[b]])
            gens = [chunk_half(c, hf, b, g, t0) for hf in range(2)]
            while gens:
                for gen in list(gens):
                    try:
                        next(gen)
                    except StopIteration:
                        gens.remove(gen)
    P.barrier()


FOX_SCALE = 128 ** -0.5


def fox_phase(P, C, o, Xin, xin_key, T):
    nc = P.nc
    _pfx = uq("t")
    NT = T // 128
    S = C.scr
    w_in = C.od_w_in[o].rearrange("(kc p) n -> p kc n", p=128)
    qTd, kTd, vd = S["qT"], S["kT"], S["vtok"]
    with ExitStack() as es:
        def A(name, shape, dt=F32):
            return Tl(es.enter_context(nc.sbuf_tensor(_pfx + name, shape, dt)))
        xs = [A("xs%d" % i, [128, D]) for i in range(2)]
        xT = A("xT", [128, 16, 512], BF16)
        wq = [A("wq%d" % i, [128, 16, 512], BF16) for i in range(2)]
        wf = A("wf", [128, 16, 16], BF16)
        ob = [A("ob%d" % i, [128, 512], BF16) for i in range(2)]
        bfb = A("bfb", [128, 16])
        tri = A("tri", [128, 128])
        ones = A("ones", [128, 128])
        onesb = A("onesb", [128, 128], BF16)
        maskb = A("maskb", [128, 128], BF16)
        spt = A("spt", [128, 16])
        CK = A("CK", [128, NT, 16])
        BASE = A("BASE", [128, NT + 1, 16])
        pp = PSPool(nc, es, _pfx, 8)
        P.dma("sync", lambda e_: e_.dma_start(out=bfb[:, :], in_=C.od_bf[o]), writes=[bfb])
        P.dma("sync", lambda e_: e_.dma_start(out=tri[:, :], in_=C.c_tri), writes=[tri])
        P.dma("sync", lambda e_: e_.dma_start(out=ones[:, :], in_=C.c_ones128), writes=[ones])
        P.dma(CQ[0], lambda e_: e_.dma_start(out=onesb[:, :], in_=C.c_ones128b), writes=[onesb])
        P.dma(CQ[0], lambda e_: e_.dma_start(out=maskb[:, :], in_=C.c_maskKQ), writes=[maskb])
        for k4 in range(4):
            P.dma(WQ[0], lambda e_, k4=k4: e_.dma_start(
                out=wf[:, k4 * 4:(k4 + 1) * 4, :], in_=w_in[:, k4 * 4:(k4 + 1) * 4, 3 * D:3 * D + 16],
                allow_slow_non_contiguous=True), writes=[wf])
        P.op("gpsimd", lambda e_: e_.memset(BASE[:, 0, :], 0.0), writes=[BASE])
        wcnt = 0
        ocnt = 0
        for g in range(T // 512):
            c0g = g * 512
            load_xT(P, C, Xin, xin_key, g, xs, xT, [pp.next() for _ in range(4)])
            for qk, dst, key in ((0, qTd, "qT"), (1, kTd, "kT")):
                for h in range(16):
                    hq, hh = divmod(h, 4)
                    if hh == 0:
                        wb = wcnt % 2
                        wcnt += 1
                        for k4 in range(4):
                            P.dma(WQ[0], lambda e_, k4=k4, wb=wb, qk=qk, hq=hq: e_.dma_start(
                                out=wq[wb][:, k4 * 4:(k4 + 1) * 4, :],
                                in_=w_in[:, k4 * 4:(k4 + 1) * 4, qk * D + hq * 512:qk * D + (hq + 1) * 512]),
                                writes=[wq[wb]])
                    ps = pp.next()
                    for kc in range(16):
                        P.op("tensor", lambda e_, ps=ps, kc=kc, wb=wb, hh=hh: e_.matmul(
                            ps[:, :], lhsT=wq[wb][:, kc, hh * 128:(hh + 1) * 128], rhs=xT[:, kc, :],
                            start=(kc == 0), stop=(kc == 15)), reads=[wq[wb], xT], writes=[ps])
                    b = ocnt % 2
                    ocnt += 1
                    P.op("scalar" if h % 2 else "vector",
                         (lambda e_, ps=ps, b=b: e_.activation(out=ob[b][:, :], in_=ps[:, :], func=AF.Copy)) if h % 2
                         else (lambda e_, ps=ps, b=b: e_.tensor_copy(out=ob[b][:, :], in_=ps[:, :])),
                         reads=[ps], writes=[ob[b]])
                    P.dma("sync", lambda e_, b=b, h=h, dst=dst, c0g=c0g: e_.dma_start(out=dst[h, :, c0g:c0g + 512], in_=ob[b][:, :]),
                          reads=[ob[b]], writes=[P.dbuf((key, g))])
            for fg in range(4):
                wb = wcnt % 2
                wcnt += 1
                for k4 in range(4):
                    P.dma(WQ[0], lambda e_, k4=k4, wb=wb, fg=fg: e_.dma_start(
                        out=wq[wb][:, k4 * 4:(k4 + 1) * 4, :],
                        in_=w_in[:, k4 * 4:(k4 + 1) * 4, 2 * D + fg * 512:2 * D + (fg + 1) * 512]), writes=[wq[wb]])
                for tt in range(4):
                    ps = pp.next()
                    for kc in range(16):
                        P.op("tensor", lambda e_, ps=ps, kc=kc, wb=wb, tt=tt: e_.matmul(
                            ps[:, :], lhsT=xT[:, kc, tt * 128:(tt + 1) * 128], rhs=wq[wb][:, kc, :],
                            start=(kc == 0), stop=(kc == 15)), reads=[wq[wb], xT], writes=[ps])
                    b = ocnt % 2
                    ocnt += 1
                    P.op("scalar" if tt % 2 else "vector",
                         (lambda e_, ps=ps, b=b: e_.activation(out=ob[b][:, :], in_=ps[:, :], func=AF.Copy)) if tt % 2
                         else (lambda e_, ps=ps, b=b: e_.tensor_copy(out=ob[b][:, :], in_=ps[:, :])),
                         reads=[ps], writes=[ob[b]])
                    r0 = c0g + tt * 128
                    P.dma("sync", lambda e_, b=b, r0=r0, fg=fg: e_.dma_start(
                        out=vd[r0:r0 + 128, fg * 512:(fg + 1) * 512], in_=ob[b][:, :]),
                        reads=[ob[b]], writes=[P.dbuf(("vtok", g))])
            for tt in range(4):
                j = g * 4 + tt
                ps = pp.next()
                for kc in range(16):
                    P.op("tensor", lambda e_, ps=ps, kc=kc, tt=tt: e_.matmul(
                        ps[:, 0:16], lhsT=xT[:, kc, tt * 128:(tt + 1) * 128], rhs=wf[:, kc, :],
                        start=(kc == 0), stop=(kc == 15)), reads=[wf, xT], writes=[ps])
                P.op("vector", lambda e_, ps=ps: e_.tensor_tensor(out=spt[:, :], in0=ps[:, 0:16], in1=bfb[:, :], op=ALU.add),
                     reads=[ps, bfb], writes=[spt])
                P.op("scalar", lambda e_: e_.activation(out=spt[:, :], in_=spt[:, :], func=AF.Exp, scale=-1.0),
                     reads=[spt], writes=[spt])
                P.op("scalar", lambda e_: e_.activation(out=spt[:, :], in_=spt[:, :], func=AF.Ln, bias=1.0),
                     reads=[spt], writes=[spt])
                ps2 = pp.next()
                P.op("tensor", lambda e_, ps2=ps2: e_.matmul(ps2[:, 0:16], lhsT=tri[:, :], rhs=spt[:, :], start=True, stop=True),
                     reads=[tri, spt], writes=[ps2])
                P.op("tensor", lambda e_, ps2=ps2: e_.matmul(ps2[:, 16:32], lhsT=ones[:, :], rhs=spt[:, :], start=True, stop=True),
                     reads=[ones, spt], writes=[ps2])
                P.op("vector", lambda e_, ps2=ps2, j=j: e_.tensor_tensor(
                    out=CK[:, j, :], in0=ps2[:, 0:16], in1=BASE[:, j, :], op=ALU.add), reads=[ps2, BASE], writes=[CK])
                P.op("vector", lambda e_, ps2=ps2, j=j: e_.tensor_tensor(
                    out=BASE[:, j + 1, :], in0=ps2[:, 16:32], in1=BASE[:, j, :], op=ALU.add), reads=[ps2, BASE], writes=[BASE])
        qh = A("qh", [128, T], BF16)
        kh = A("kh", [128, T], BF16)
        vh = A("vh", [128, NT, 128], BF16)
        pT = [A("pT%d" % i, [128, 512], BF16) for i in range(2)]
        bia = [A("bia%d" % i, [128, 4]) for i in range(2)]
        rden = A("rden", [128, 512])
        oTb = A("oTb", [128, 512], BF16)
        allg = list(range(T // 512))
        vview = vd.rearrange("(j p) n -> p j n", p=128)
        pcnt = 0
        ocnt2 = 0
        for h in range(16):
            P.dma("sync", lambda e_, h=h: e_.dma_start(out=qh[:, :], in_=qTd[h, :, :]),
                  reads=[P.dbuf(("qT", g)) for g in allg], writes=[qh])
            P.dma("sync", lambda e_, h=h: e_.dma_start(out=kh[:, :], in_=kTd[h, :, :]),
                  reads=[P.dbuf(("kT", g)) for g in allg], writes=[kh])
            P.dma("sync", lambda e_, h=h: e_.dma_start(out=vh[:, :, :], in_=vview[:, :, h * 128:(h + 1) * 128]),
                  reads=[P.dbuf(("vtok", g)) for g in allg], writes=[vh])
            for G in range(T // 512):
                po, pden = pp.t[4 + 2 * (ocnt2 % 2)], pp.t[5 + 2 * (ocnt2 % 2)]
                ocnt2 += 1
                nkb = 4 * G + 4
                for kb in range(nkb):
                    ps = pp.t[pcnt % 4]
                    P.op("tensor", lambda e_, ps=ps, kb=kb, G=G: e_.matmul(
                        ps[:, :], lhsT=kh[:, kb * 128:(kb + 1) * 128], rhs=qh[:, G * 512:(G + 1) * 512],
                        start=True, stop=True), reads=[kh, qh], writes=[ps])
                    b = pcnt % 2
                    pcnt += 1
                    P.op("vector", lambda e_, b=b, kb=kb, G=G, h=h: e_.tensor_scalar(
                        out=bia[b][:, :], in0=BASE[:, 4 * G + 1:4 * G + 5, h], scalar1=-1.0,
                        scalar2=CK[:, kb, h:h + 1], op0=ALU.mult, op1=ALU.add), reads=[BASE, CK], writes=[bia[b]])
                    for j in range(4):
                        qb = 4 * G + j
                        cs = slice(j * 128, (j + 1) * 128)
                        if kb > qb:
                            P.op("gpsimd", lambda e_, b=b, cs=cs: e_.memset(pT[b][:, cs], 0.0), writes=[pT[b]])
                            continue
                        P.op("scalar", lambda e_, ps=ps, b=b, cs=cs, j=j: e_.activation(
                            out=pT[b][:, cs], in_=ps[:, cs], func=AF.Exp, bias=bia[b][:, j:j + 1], scale=FOX_SCALE),
                            reads=[ps, bia[b]], writes=[pT[b]], skip_self=True)
                        if kb == qb:
                            P.op("gpsimd", lambda e_, b=b, cs=cs: e_.tensor_tensor(
                                out=pT[b][:, cs], in0=pT[b][:, cs], in1=maskb[:, :], op=ALU.mult),
                                reads=[pT[b], maskb], writes=[pT[b]])
                    P.op("tensor", lambda e_, b=b, kb=kb, nkb=nkb, po=po: e_.matmul(
                        po[:, :], lhsT=vh[:, kb, :], rhs=pT[b][:, :], start=(kb == 0), stop=(kb == nkb - 1)),
                        reads=[vh, pT[b]], writes=[po])
                    P.op("tensor", lambda e_, b=b, kb=kb, nkb=nkb, pden=pden: e_.matmul(
                        pden[:, :], lhsT=onesb[:, :], rhs=pT[b][:, :], start=(kb == 0), stop=(kb == nkb - 1)),
                        reads=[onesb, pT[b]], writes=[pden])
                P.op("vector", lambda e_, pden=pden: e_.reciprocal(out=rden[:, :], in_=pden[:, :]), reads=[pden], writes=[rden])
                P.op("vector", lambda e_, po=po: e_.tensor_tensor(out=oTb[:, :], in0=po[:, :], in1=rden[:, :], op=ALU.mult),
                     reads=[po, rden], writes=[oTb])
                P.dma("sync", lambda e_, h=h, G=G: e_.dma_start(
                    out=S["catT"][h * 128:(h + 1) * 128, G * 512:(G + 1) * 512], in_=oTb[:, :]),
                    reads=[oTb], writes=[P.dbuf(("catT_f", G))])
    P.barrier()


def build(T, layers, mode="full", dbg_out=()):
    nc = bass.Bass("TRN2", target_bir_lowering=False)
    C = Ctx()
    NCH = T // 64

    def inp(name, shape, dt=F32):
        return nc.dram_tensor(name, list(shape), dt, kind="ExternalInput").ap()

    def scr(name, shape, dt=F32):
        return nc.dram_tensor(name, list(shape), dt, kind="Internal").ap()

    C.x = inp("x", [T, D])
    identd = inp("ident", [128, 128])
    C.ln_g = inp("ln_g", [DEPTH, 2, 128, D])
    C.ln_b = inp("ln_b", [DEPTH, 2, 128, D])
    C.ff_w_up = {l: inp("ff_w_up%d" % l, [D, 2 * DFF], WDT[0]) for l in layers}
    C.ff_w_down = {l: inp("ff_w_down%d" % l, [DFF, D], WDT[0]) for l in layers}
    C.ff_cwb = inp("ff_cwb", [DEPTH, 128, 4, 2 * NFC])
    evs = sorted({l // 2 for l in layers if l % 2 == 0})
    ods = sorted({l // 2 for l in layers if l % 2 == 1})
    if mode != "ffn_only":
        C.c_ones64 = inp("c_ones64", [64, 64])
        C.c_rst = inp("c_rst", [64, 512])
        C.c_maskL = inp("c_maskL", [64, 64])
        C.c_mask2 = inp("c_mask2", [64, 2, 64])
        C.c_id64 = inp("c_id64", [64, 64])
        C.c_tri = inp("c_tri", [128, 128])
        C.c_ones128 = inp("c_ones128", [128, 128])
        C.c_ones128b = inp("c_ones128b", [128, 128], WDT[0])
        C.c_maskKQ = inp("c_maskKQ", [128, 128], WDT[0])
        C.ev_w_in = {e: inp("ev_w_in%d" % e, [D, 6432], WDT[0]) for e in evs}
        C.ev_w_out = {e: inp("ev_w_out%d" % e, [D, D], WDT[0]) for e in evs}
        C.ev_pp = {e: inp("ev_pp%d" % e, [64, NPP]) for e in evs}
        C.ev_w2 = {e: inp("ev_w2_%d" % e, [64, 1024], WDT[0]) for e in evs}
        C.ev_a2 = {e: inp("ev_a2_%d" % e, [64, 1024], WDT[0]) for e in evs}
        C.ev_g2 = {e: inp("ev_g2_%d" % e, [64, 3, 1024], WDT[0]) for e in evs}
        C.ev_rkmat = {e: inp("ev_rkmat%d" % e, [64, 16, 64]) for e in evs}
        C.ev_cw = {e: inp("ev_cw%d" % e, [128, 3, 8]) for e in evs}
        C.ev_v1 = {e - 1: inp("ev_v1_%d" % (e - 1), [64, 16, 32]) for e in evs if e > 0}
        C.ev_v2 = {e - 1: inp("ev_v2_%d" % (e - 1), [32, 1024]) for e in evs if e > 0}
        C.od_w_in = {o: inp("od_w_in%d" % o, [D, 3 * D + 16], WDT[0]) for o in ods}
        C.od_w_out = {o: inp("od_w_out%d" % o, [D, D], WDT[0]) for o in ods}
        C.od_bf = {o: inp("od_bf%d" % o, [128, 16]) for o in ods}
        S = {}
        S["catT"] = scr("catT", [D, T], BF16)
        if evs:
            for n in ("vfirst", "gT", "bonT", "bt", "kt"):
                S[n] = scr(n, [16, 64, T])
            S["ar"] = scr("ar", [16, 64, NCH, 2, 64])
            for q in range(3):
                S["tok%d" % q] = scr("tok%d" % q, [16, 64, NCH, 64])
            S["wc"] = scr("wc", [16, 64, NCH])
        if ods:
            S["qT"] = scr("qT", [16, 128, T], BF16)
            S["kT"] = scr("kT", [16, 128, T], BF16)
            S["vtok"] = scr("vtok", [T, D], BF16)
        C.scr = S
    out = nc.dram_tensor("out", [T, D], F32, kind="ExternalOutput").ap()
    xa = scr("xa", [T, D])
    xb = scr("xb", [T, D])
    P = Prog(nc)
    C.dbg = None
    if mode == "ffn_only":
        C.dbg = {"xT": nc.dram_tensor("d_xT", [128, 16, 512], BF16, kind="ExternalOutput").ap(),
                 "hT": nc.dram_tensor("d_hT", [128, NFC, 512], BF16, kind="ExternalOutput").ap(),
                 "z": nc.dram_tensor("d_z", [128, D], F32, kind="ExternalOutput").ap()}
    with ExitStack() as es:
        C.ident = Tl(es.enter_context(nc.sbuf_tensor("ident_sb", [128, 128], F32)))
        P.dma("sync", lambda e: e.dma_start(out=C.ident[:, :], in_=identd), writes=[C.ident])
        if WDT[0] == F32:
            pairs = []
            dicts = [C.ff_w_up, C.ff_w_down]
            if mode != "ffn_only":
                dicts += [C.ev_w_in, C.ev_w_out, C.od_w_in, C.od_w_out]
            for di, dct in enumerate(dicts):
                for key in sorted(dct):
                    src = dct[key]
                    dst = scr("wbf_%d_%d" % (di, key), list(src.shape), BF16)
                    pairs.append((src, dst))
                    dct[key] = dst
            precast_phase(P, C, pairs)
        if mode == "ffn_only":
            ffn_phase(P, C, layers[0], C.x, "x", out, "out", T)
        else:
            cur, ckey = C.x, "x"
            for li, l in enumerate(layers):
                last = li == len(layers) - 1
                if l % 2 == 0:
                    e = l // 2
                    rwkv_prep_phase(P, C, e, l, cur, ckey, T)
                    if mode == "prep_only":
                        break
                    rwkv_scan_phase(P, C, e, T)
                    if mode == "scan_only":
                        break
                    srckeys, w_out = "cat_e%d" % l, C.ev_w_out[e]
                else:
                    o = l // 2
                    fox_phase(P, C, o, cur, ckey, T)
                    w_out = C.od_w_out[o]
                if mode == "mixer_only":
                    outproj_phase(P, C, C.scr["catT"], None, w_out, l, cur, ckey, out, "out", T)
                    break
                outproj_phase(P, C, C.scr["catT"], None, w_out, l, cur, ckey, xa, "xa%d" % l, T)
                dst, dkey = (out, "out") if last else (xb, "xb%d" % l)
                ffn_phase(P, C, l, xa, "xa%d" % l, dst, dkey, T)
                cur, ckey = dst, dkey
        for n in dbg_out:
            src = C.scr[n]
            dst = nc.dram_tensor("dbg_" + n, list(src.shape), src.tensor.dtype, kind="ExternalOutput").ap()
            P.dma("sync", lambda e, src=src, dst=dst: e.dma_start(out=dst, in_=src))
        P.emit()
    return nc


def host_inputs(inputs, T, layers=(0, 1, 2, 3), mode="full"):
    f = np.float32
    m = {}
    m["ident"] = np.eye(128, dtype=f)
    m["ln_g"] = np.ascontiguousarray(np.broadcast_to(inputs["ln_g"][:, :, None, :], (DEPTH, 2, 128, D))).astype(f)
    m["ln_b"] = np.ascontiguousarray(np.broadcast_to(inputs["ln_b"][:, :, None, :], (DEPTH, 2, 128, D))).astype(f)
    for l in layers:
        m["ff_w_up%d" % l] = np.ascontiguousarray(inputs["ff_w_up"][l], dtype=f)
        m["ff_w_down%d" % l] = np.ascontiguousarray(inputs["ff_w_down"][l], dtype=f)
    cw = np.concatenate([inputs["ff_conv_w"], inputs["ff_conv_b"][:, None, :]], axis=1)
    m["ff_cwb"] = np.ascontiguousarray(cw.reshape(DEPTH, 4, 2 * NFC, 128).transpose(0, 3, 1, 2)).astype(f)
    if mode == "ffn_only":
        return m
    p = np.arange(64)
    m["c_ones64"] = np.ones((64, 64), f)
    rst = np.ones((64, 512), f)
    rst[:, ::64] = 0.0
    m["c_rst"] = rst
    m["c_maskL"] = (p[:, None] > p[None, :]).astype(f)
    m["c_mask2"] = np.ascontiguousarray(np.stack([(p[:, None] < p[None, :]), (p[:, None] <= p[None, :])], 1)).astype(f)
    m["c_id64"] = np.eye(64, dtype=f)
    q = np.arange(128)
    m["c_tri"] = (q[:, None] <= q[None, :]).astype(f)
    m["c_ones128"] = np.ones((128, 128), f)
    m["c_ones128b"] = np.ones((128, 128), f)
    m["c_maskKQ"] = (q[:, None] <= q[None, :]).astype(f)
    hd = lambda v: np.ascontiguousarray(np.asarray(v, f).reshape(16, 64).T)
    for e in sorted({l // 2 for l in layers if l % 2 == 0}):
        m["ev_w_in%d" % e] = np.ascontiguousarray(inputs["ev_w_in"][e], dtype=f)
        m["ev_w_out%d" % e] = np.ascontiguousarray(inputs["ev_w_out"][e], dtype=f)
        mu = np.asarray(inputs["ev_mu"][e], f)
        pp = np.zeros((64, NPP), f)
        pp[:, PP_MUR:PP_MUR + 16] = hd(mu[0:1024])
        pp[:, PP_MUK:PP_MUK + 16] = hd(mu[1024:2048])
        pp[:, PP_MUV:PP_MUV + 16] = hd(mu[2048:3072])
        pp[:, PP_W0:PP_W0 + 16] = hd(inputs["ev_w0"][e])
        pp[:, PP_A0:PP_A0 + 16] = hd(inputs["ev_a0"][e])
        pp[:, PP_KK:PP_KK + 16] = hd(inputs["ev_k_k"][e])
        pp[:, PP_KA:PP_KA + 16] = hd(inputs["ev_k_a"][e])
        pp[:, PP_LG:PP_LG + 16] = hd(inputs["ev_lnx_g"][e])
        pp[:, PP_LB:PP_LB + 16] = hd(inputs["ev_lnx_b"][e])
        if e > 0:
            pp[:, PP_V0:PP_V0 + 16] = hd(inputs["ev_v0"][e - 1])
        pp[:, PP_MUDW] = mu[3072:3136]
        pp[:, PP_MUDA] = mu[3136:3200]
        pp[:, PP_MUDG] = mu[3200:3264]
        pp[:, PP_MUDG + 1] = mu[3264:3328]
        pp[:32, PP_MUDG + 2] = mu[3328:3360]
        m["ev_pp%d" % e] = pp
        m["ev_w2_%d" % e] = np.ascontiguousarray(inputs["ev_w2"][e], dtype=f)
        m["ev_a2_%d" % e] = np.ascontiguousarray(inputs["ev_a2"][e], dtype=f)
        g2 = np.zeros((192, 1024), f)
        g2[:160] = inputs["ev_g2"][e]
        m["ev_g2_%d" % e] = np.ascontiguousarray(g2.reshape(3, 64, 1024).transpose(1, 0, 2))
        rk = np.asarray(inputs["ev_r_k"][e], f)
        m["ev_rkmat%d" % e] = np.ascontiguousarray(np.broadcast_to(rk.T[:, :, None], (64, 16, 64))).astype(f)
        cwv = np.asarray(inputs["ev_conv_w"][e], f)
        m["ev_cw%d" % e] = np.ascontiguousarray(cwv.reshape(3, 8, 128).transpose(2, 0, 1))
        if e > 0:
            v1 = np.asarray(inputs["ev_v1"][e - 1], f)
            m["ev_v1_%d" % (e - 1)] = np.ascontiguousarray(v1.reshape(16, 64, 32).transpose(1, 0, 2))
            m["ev_v2_%d" % (e - 1)] = np.ascontiguousarray(inputs["ev_v2"][e - 1], dtype=f)
    for o in sorted({l // 2 for l in layers if l % 2 == 1}):
        m["od_w_in%d" % o] = np.ascontiguousarray(inputs["od_w_in"][o], dtype=f)
        m["od_w_out%d" % o] = np.ascontiguousarray(inputs["od_w_out"][o], dtype=f)
        m["od_bf%d" % o] = np.ascontiguousarray(np.broadcast_to(np.asarray(inputs["od_b_f"][o], f)[None, :], (128, 16)))
    return m


SEQ = 4096
N_ACTIVE = 4


def kernel(**inputs):
    x = np.asarray(inputs["x"], np.float32)
    B = x.shape[0]
    nc = build(SEQ, [0, 1, 2, 3])
    base = host_inputs(inputs, SEQ)
    in_maps = []
    for c in range(N_ACTIVE):
        m = dict(base)
        m["x"] = np.ascontiguousarray(x[c % B])
        in_maps.append(m)
    res = run_bass_kernel_spmd(nc, in_maps, core_ids=list(range(N_ACTIVE)))
    out = np.stack([np.asarray(res.results[b]["out"], np.float32) for b in range(B)], 0)
    return out
```
